# Optimizing a Trainium2 kernel written in Bass

```python
import math
import jax, jax.numpy as jnp
from jax import lax
import numpy as np

D_MODEL = 1024
BATCH = 8
SEQ = 4096
DEPTH = 2

GRID_W = 64
N_MIXERS = 2
NA_HEADS = 16
NA_HEAD_DIM = D_MODEL // NA_HEADS
NA_KH = 8
NA_KW = 16
DA_HEADS = 8
DA_HEAD_DIM = D_MODEL // (2 * DA_HEADS)
Q_BLOCK = 128
T5_BUCKETS = 32
T5_MAX_DIST = 128
D_FF = 2816
CONV_WIDTH = 3
PLE_DIM = 256
N_NORMS = 5
N_A_LAYERS = (DEPTH + 1) // 2
N_B_LAYERS = DEPTH // 2
EPS = 1e-6

kernel_name = "hybrid_natten_diffattn_convglu_encoder"


def rms_norm(x, g):
    x32 = x.astype(jnp.float32)
    y = x32 * lax.rsqrt(jnp.mean(x32 * x32, axis=-1, keepdims=True) + EPS)
    return (y * g.astype(jnp.float32)).astype(x.dtype)


def t5_bucket(rel):
    half = T5_BUCKETS // 2
    max_exact = half // 2
    sign_off = jnp.where(rel > 0, half, 0)
    n = jnp.abs(rel)
    nf = jnp.maximum(n, 1).astype(jnp.float32)
    large = max_exact + (jnp.log(nf / max_exact) / math.log(T5_MAX_DIST / max_exact)
                         * (half - max_exact)).astype(jnp.int32)
    large = jnp.minimum(large, half - 1)
    return sign_off + jnp.where(n < max_exact, n, large)


def neighborhood_attention(h, w_qkv, rpb, w_o):
    B, S, D = h.shape
    rows = S // GRID_W
    kh = min(NA_KH, rows)
    qkv = (h @ w_qkv).reshape(B, rows, GRID_W, 3, NA_HEADS, NA_HEAD_DIM)
    qkv = qkv.transpose(3, 0, 4, 1, 2, 5)
    q = qkv[0] * (NA_HEAD_DIM ** -0.5)
    k = qkv[1]
    v = qkv[2]
    cols = jnp.arange(GRID_W)
    col_idx = jnp.clip(cols - NA_KW // 2, 0, GRID_W - NA_KW)[:, None] + jnp.arange(NA_KW)[None, :]
    col_off = col_idx - cols[:, None] + (NA_KW - 1)
    rpb_c = rpb[:, :, col_off]

    def row_fn(r):
        r0 = jnp.clip(r - kh // 2, 0, rows - kh)
        k_rows = lax.dynamic_slice_in_dim(k, r0, kh, axis=2)
        v_rows = lax.dynamic_slice_in_dim(v, r0, kh, axis=2)
        k_win = k_rows[:, :, :, col_idx]
        v_win = v_rows[:, :, :, col_idx]
        q_r = lax.dynamic_index_in_dim(q, r, axis=2, keepdims=False)
        s = jnp.einsum('bhcd,bhicjd->bhcij', q_r, k_win).astype(jnp.float32)
        row_off = r0 + jnp.arange(kh) - r + (NA_KH - 1)
        bias = rpb_c[:, row_off].transpose(0, 2, 1, 3)
        s = s + bias[None].astype(jnp.float32)
        a = jax.nn.softmax(s.reshape(B, NA_HEADS, GRID_W, kh * NA_KW), axis=-1).reshape(s.shape)
        return jnp.einsum('bhcij,bhicjd->bhcd', a.astype(v.dtype), v_win)

    o = lax.map(row_fn, jnp.arange(rows))
    o = o.transpose(1, 0, 3, 2, 4).reshape(B, S, D)
    return o @ w_o


def diff_attention(h, w_qkv, lam, subln_g, w_o, t5_table, lambda_init):
    B, S, D = h.shape
    nb = S // Q_BLOCK
    q, k, v = jnp.split(h @ w_qkv, 3, axis=-1)
    q = q.reshape(B, S, DA_HEADS, 2, DA_HEAD_DIM).transpose(0, 2, 3, 1, 4) * (DA_HEAD_DIM ** -0.5)
    k = k.reshape(B, S, DA_HEADS, 2, DA_HEAD_DIM).transpose(0, 2, 3, 1, 4)
    v = v.reshape(B, S, DA_HEADS, 2 * DA_HEAD_DIM).transpose(0, 2, 1, 3)
    lam32 = lam.astype(jnp.float32)
    lam_full = (jnp.exp(jnp.sum(lam32[0] * lam32[1])) - jnp.exp(jnp.sum(lam32[2] * lam32[3]))
                + lambda_init)
    q_blocks = q.reshape(B, DA_HEADS, 2, nb, Q_BLOCK, DA_HEAD_DIM).transpose(3, 0, 1, 2, 4, 5)
    kpos = jnp.arange(S)

    def block_fn(args):
        qb, i = args
        qpos = i * Q_BLOCK + jnp.arange(Q_BLOCK)
        bias = t5_table[t5_bucket(kpos[None, :] - qpos[:, None])]
        bias = bias.transpose(2, 0, 1).astype(jnp.float32)
        s = jnp.einsum('bhmqd,bhmkd->bhmqk', qb, k).astype(jnp.float32) + bias[None, :, None]
        a = jax.nn.softmax(s, axis=-1)
        w = a[:, :, 0] - lam_full * a[:, :, 1]
        return jnp.einsum('bhqk,bhkd->bhqd', w.astype(v.dtype), v)

    o = lax.map(block_fn, (q_blocks, jnp.arange(nb)))
    o = o.transpose(1, 2, 0, 3, 4).reshape(B, DA_HEADS, S, 2 * DA_HEAD_DIM)
    o = rms_norm(o, subln_g) * (1.0 - lambda_init)
    o = o.transpose(0, 2, 1, 3).reshape(B, S, D)
    return o @ w_o


def conv_glu_ffn(h, w_in, conv_w, conv_b, w_out):
    u = h @ w_in
    gate, val = jnp.split(u, 2, axis=-1)
    gate = lax.conv_general_dilated(
        gate, conv_w[:, None, :].astype(gate.dtype), window_strides=(1,),
        padding=((CONV_WIDTH // 2, CONV_WIDTH // 2),),
        dimension_numbers=('NWC', 'WIO', 'NWC'), feature_group_count=D_FF) + conv_b
    return (jax.nn.gelu(gate, approximate=True) * val) @ w_out


def setup_inputs(seed: int = 0) -> dict:
    key = jax.random.key(seed)
    ks = jax.random.split(key, 20)
    f32 = jnp.float32
    D, F = D_MODEL, D_FF
    nrm = lambda k, s: jax.random.normal(k, s, f32)
    return {
        'x': nrm(ks[0], (BATCH, SEQ, D)),
        'p': nrm(ks[1], (DEPTH, BATCH, SEQ, PLE_DIM)),
        'norm_g': 1.0 + 0.05 * nrm(ks[2], (DEPTH, N_NORMS, D)),
        'na_w_qkv': nrm(ks[3], (N_A_LAYERS, D, 3 * D)) * D ** -0.5,
        'na_rpb': 0.1 * nrm(ks[4], (N_A_LAYERS, NA_HEADS, 2 * NA_KH - 1, 2 * NA_KW - 1)),
        'na_w_o': nrm(ks[5], (N_A_LAYERS, D, D)) * D ** -0.5,
        'da_w_qkv': nrm(ks[6], (N_B_LAYERS, D, 3 * D)) * D ** -0.5,
        'da_lambda': 0.1 * nrm(ks[7], (N_B_LAYERS, 4, DA_HEAD_DIM)),
        'da_subln_g': 1.0 + 0.05 * nrm(ks[8], (N_B_LAYERS, 2 * DA_HEAD_DIM)),
        'da_w_o': nrm(ks[9], (N_B_LAYERS, D, D)) * D ** -0.5,
        't5_table': 0.1 * nrm(ks[10], (T5_BUCKETS, DA_HEADS)),
        'ffn_w_in': nrm(ks[11], (DEPTH, D, 2 * F)) * D ** -0.5,
        'ffn_conv_w': nrm(ks[12], (DEPTH, CONV_WIDTH, F)) * CONV_WIDTH ** -0.5,
        'ffn_conv_b': 0.02 * nrm(ks[13], (DEPTH, F)),
        'ffn_w_out': nrm(ks[14], (DEPTH, F, D)) * F ** -0.5,
        'ple_w_gate': nrm(ks[15], (DEPTH, D, D)) * D ** -0.5,
        'ple_w_proj': nrm(ks[16], (DEPTH, PLE_DIM, D)) * PLE_DIM ** -0.5,
    }


def reference(x, p, norm_g, na_w_qkv, na_rpb, na_w_o, da_w_qkv, da_lambda, da_subln_g,
              da_w_o, t5_table, ffn_w_in, ffn_conv_w, ffn_conv_b, ffn_w_out,
              ple_w_gate, ple_w_proj):
    for i in range(DEPTH):
        g = norm_g[i]
        h = rms_norm(x, g[0])
        j = i // N_MIXERS
        if i % N_MIXERS == 0:
            m = neighborhood_attention(h, na_w_qkv[j], na_rpb[j], na_w_o[j])
        else:
            lambda_init = 0.8 - 0.6 * math.exp(-0.3 * i)
            m = diff_attention(h, da_w_qkv[j], da_lambda[j], da_subln_g[j], da_w_o[j],
                               t5_table, lambda_init)
        x = x + rms_norm(m, g[1])
        f = conv_glu_ffn(rms_norm(x, g[2]), ffn_w_in[i], ffn_conv_w[i], ffn_conv_b[i], ffn_w_out[i])
        x = x + rms_norm(f, g[3])
        gate = jax.nn.sigmoid(rms_norm(x, g[4]) @ ple_w_gate[i])
        x = x + gate * (p[i] @ ple_w_proj[i])
    return x
```

```python
import math
import numpy as np
import concourse.bass as bass
import concourse.mybir as mybir
from concourse.bass_utils import run_bass_kernel_spmd

F32 = mybir.dt.float32
BF16 = mybir.dt.bfloat16
AF = mybir.ActivationFunctionType
ALU = mybir.AluOpType

S = 4096
D = 1024
NT = 32
NB = 8
KC = 8
FF = 2816
FC = 22
EPS = 1e-6
NEG = -30000.0
LAMBDA_INIT1 = 0.8 - 0.6 * math.exp(-0.3 * 1)
FGROUPS = [(0, 6), (6, 6), (12, 5), (17, 5)]
DA_DEPTH = 3
DA_SUM_EVERY = 2
DA_PT = 8
DA_SKIP_DVE = False


class DSem:
    def __init__(self, h):
        self.h = h
        self.count = 0


class Buf:
    def __init__(self, name, t=None):
        self.name = name
        self.t = t
        self.last_w = None
        self.readers = []
        self.dsem = None


class Ctx:
    def __init__(self, nc, n_dsems=40):
        self.nc = nc
        self.engs = {"pe": nc.tensor, "act": nc.scalar, "dve": nc.vector, "pool": nc.gpsimd, "sp": nc.sync}
        self.tl = {k: DSem(nc.alloc_semaphore(name="tl_" + k)) for k in self.engs}
        self.seen = {k: {} for k in self.engs}
        self.free_dsems = [DSem(nc.alloc_semaphore(name="d%d" % i)) for i in range(n_dsems)]
        self.all_dsems = list(self.free_dsems)
        self.live = []
        self.ninst = {k: 0 for k in self.engs}

    def push(self):
        self.live.append([])

    def pop(self):
        self.barrier()
        for cm, b in reversed(self.live.pop()):
            if b.dsem is not None:
                self.free_dsems.append(b.dsem)
            if cm is not None:
                cm.__exit__(None, None, None)

    def sb(self, name, shape, dt, dma=False):
        self.uid = getattr(self, "uid", 0) + 1
        name = "s%d_%s" % (self.uid, name)
        cm = self.nc.sbuf_tensor(name, list(shape), dt)
        b = Buf(name, cm.__enter__())
        b.shape = list(shape)
        if dma:
            b.dsem = self.free_dsems.pop()
        self.live[-1].append((cm, b))
        return b

    def ps(self, name, shape, dt=F32):
        self.uid = getattr(self, "uid", 0) + 1
        name = "p%d_%s" % (self.uid, name)
        cm = self.nc.psum_tensor(name, list(shape), dt)
        b = Buf(name, cm.__enter__())
        self.live[-1].append((cm, b))
        return b

    def dram(self, name):
        b = Buf(name)
        return b

    def ring(self, name, n, shape, dt, dma=False, psum=False):
        if psum:
            return Ring([self.ps("%s%d" % (name, i), shape, dt) for i in range(n)])
        return Ring([self.sb("%s%d" % (name, i), shape, dt, dma=dma) for i in range(n)])

    def _waits(self, eng, reads, writes):
        need = {}

        def add(tok):
            if tok is None:
                return
            s, v = tok
            if eng == "pe" and s is self.tl["pe"]:
                return
            if need.get(s, 0) < v:
                need[s] = v

        for b in reads:
            add(b.last_w)
        for b in writes:
            add(b.last_w)
            for r in b.readers:
                add(r)
        e = self.engs[eng]
        seen = self.seen[eng]
        for s, v in need.items():
            if seen.get(s, 0) < v:
                e.wait_ge(s.h, v)
                seen[s] = v

    def _record(self, tok, reads, writes):
        for b in reads:
            b.readers.append(tok)
            if len(b.readers) > 16:
                m = {}
                for s, v in b.readers:
                    if m.get(s, 0) < v:
                        m[s] = v
                b.readers = list(m.items())
        for b in writes:
            b.last_w = tok
            b.readers = []

    def op(self, eng, fn, reads=(), writes=(), inc=True):
        self._waits(eng, reads, writes)
        ins = fn(self.engs[eng])
        tl = self.tl[eng]
        if inc:
            tl.count += 1
            ins.then_inc(tl.h, 1)
            tok = (tl, tl.count)
        else:
            tok = (tl, tl.count + 1)
        self.ninst[eng] += 1
        self._record(tok, reads, writes)
        return ins

    def dma(self, q, out_ap, in_ap, owner, reads=(), writes=(), **kw):
        self._waits(q, reads, writes)
        ins = self.engs[q].dma_start(out=out_ap, in_=in_ap, **kw)
        ds = owner.dsem
        ds.count += 16
        ins.then_inc(ds.h, 16)
        tok = (ds, ds.count)
        self.ninst[q] += 1
        self._record(tok, reads, writes)
        return ins

    def barrier(self):
        sems = [self.tl[k] for k in ("pe", "act", "dve", "pool")] + self.all_dsems
        for k in self.engs:
            e = self.engs[k]
            seen = self.seen[k]
            for s in sems:
                if s.count > 0 and seen.get(s, 0) < s.count:
                    e.wait_ge(s.h, s.count)
                    seen[s] = s.count


class Ring:
    def __init__(self, bufs):
        self.bufs = bufs
        self.i = 0

    def next(self):
        b = self.bufs[self.i % len(self.bufs)]
        self.i += 1
        return b


def na_tile_list():
    tiles = [(7, c) for c in range(12, 18)]
    tiles += [(0, c) for c in range(4)] + [(15, c) for c in range(28, 32)]
    return tiles


def na_chunks(P):
    lo_row = min(max(4 * P - 4, 0), 56)
    hi_row = min(max(4 * P - 1, 0), 56) + 7
    return list(range(lo_row // 2, hi_row // 2 + 1))


def na_tile_id(P, c):
    if 1 <= P <= 14:
        return c - 2 * P + 2
    return 6 + c if P == 0 else 10 + (c - 28)


NA_NT = 14


def na_index_table():
    tiles = na_tile_list()
    idx = np.zeros((len(tiles), 128, 256), np.int64)
    kk = np.arange(128)[:, None]
    qq = np.arange(256)[None, :]
    for t, (P, c) in enumerate(tiles):
        krow = 2 * c + kk // 64
        kcol = kk % 64
        qrow = 4 * P + qq // 64
        qcol = qq % 64
        r0 = np.clip(qrow - 4, 0, 56)
        c0 = np.clip(qcol - 8, 0, 48)
        valid = (krow >= r0) & (krow < r0 + 8) & (kcol >= c0) & (kcol < c0 + 16)
        ro = krow - qrow + 7
        co = kcol - qcol + 15
        idx[t] = np.where(valid, ro * 31 + co, 15 * 31)
    return idx


def t5_bucket_np(rel):
    nn = np.abs(rel)
    nf = np.maximum(nn, 1).astype(np.float32)
    lg = (np.log(nf / np.float32(8)) / np.float32(math.log(16)) * np.float32(8)).astype(np.int32) + 8
    lg = np.minimum(lg, 15)
    return np.where(rel > 0, 16, 0) + np.where(nn < 8, nn, lg)


def ktile(w):
    K, N = w.shape
    return np.ascontiguousarray(w.reshape(K // 128, 128, N).transpose(1, 0, 2))


def prep_common(inp):
    f = np.float32
    c = {}
    g = inp["norm_g"]
    c["gT"] = np.ascontiguousarray(g.reshape(2, 5, 8, 128).transpose(3, 0, 1, 2)).astype(f)
    c["gB"] = np.ascontiguousarray(g[:, [1, 3], :]).astype(f)
    c["wqkv0"] = ktile(inp["na_w_qkv"][0])
    c["wqkv1"] = ktile(inp["da_w_qkv"][0])
    c["wo0"] = ktile(inp["na_w_o"][0])
    c["wo1"] = ktile(inp["da_w_o"][0])
    c["win"] = np.stack([ktile(inp["ffn_w_in"][l]) for l in range(2)])
    c["wout"] = np.stack([ktile(inp["ffn_w_out"][l]) for l in range(2)])
    c["wg"] = np.stack([ktile(inp["ple_w_gate"][l]) for l in range(2)])
    c["wp"] = np.stack([ktile(inp["ple_w_proj"][l]) for l in range(2)])
    cw = inp["ffn_conv_w"]
    c["cw"] = np.ascontiguousarray(cw.reshape(2, 3, FC, 128).transpose(3, 0, 1, 2)).astype(f)
    c["cb"] = np.ascontiguousarray(inp["ffn_conv_b"].reshape(2, FC, 128).transpose(2, 0, 1)).astype(f)
    rpb = inp["na_rpb"][0].reshape(16, 15 * 31)
    rpb_ext = np.concatenate([rpb, np.full((16, 1), NEG, f)], axis=1)
    idx = na_index_table()
    nab = rpb_ext[:, idx]
    c["nab"] = np.ascontiguousarray(nab.transpose(0, 2, 1, 3)).astype(f)
    kk = np.arange(128)[:, None]
    u = np.arange(1152)[None, :]
    bk = t5_bucket_np(kk - u + 512)
    tab = inp["t5_table"]
    c["t5s"] = np.ascontiguousarray(tab[bk].transpose(2, 0, 1)).astype(f)
    c["t5c"] = np.ascontiguousarray(np.broadcast_to(tab[[15, 31], :].T[None], (128, 8, 2))).astype(f)
    c["lam"] = np.ascontiguousarray(np.broadcast_to(inp["da_lambda"][0][None], (128, 4, 64))).astype(f)
    c["sg"] = np.ascontiguousarray(inp["da_subln_g"][0].reshape(128, 1)).astype(f)
    return c


COMMON_SHAPES = {
    "gT": [128, 2, 5, 8], "gB": [2, 2, D], "wqkv0": [128, KC, 3 * D], "wqkv1": [128, KC, 3 * D],
    "wo0": [128, KC, D], "wo1": [128, KC, D], "win": [2, 128, KC, 2 * FF], "wout": [2, 128, FC, D],
    "wg": [2, 128, KC, D], "wp": [2, 128, 2, D], "cw": [128, 2, 3, FC], "cb": [128, 2, FC],
    "nab": [16, 128, NA_NT, 256], "t5s": [8, 128, 1152], "t5c": [128, 8, 2], "lam": [128, 4, 64], "sg": [128, 1],
}


class Prog:
    def __init__(self, debug=False, stop_after=None, layers=(0, 1), da_heads=8, na_pairs=8):
        self.debug = debug
        self.stop_after = stop_after
        self.layers = layers
        self.da_heads = da_heads
        self.na_pairs = na_pairs
        nc = self.nc = bass.Bass("TRN2", target_bir_lowering=False)
        self.c = Ctx(nc)
        di = lambda n, s, dt=F32: nc.dram_tensor(n, list(s), dt, kind="ExternalInput").ap()
        self.x = di("x", [S, D])
        self.pT = di("pT", [2, 128, 2, S])
        self.w = {k: di(k, s) for k, s in COMMON_SHAPES.items()}
        self.y = nc.dram_tensor("y", [S, D], F32, kind="ExternalOutput").ap()
        sk = "ExternalOutput" if debug else "Internal"
        self.xs1 = nc.dram_tensor("xs1", [S, D], F32, kind=sk).ap()
        self.xs2 = nc.dram_tensor("xs2", [S, D], F32, kind=sk).ap()
        self.oT = nc.dram_tensor("oT", [8, 128, S], BF16, kind=sk).ap()
        self.aT = nc.dram_tensor("aT", [FC, 128, S], BF16, kind=sk).ap()
        c = self.c
        self.d_xs1 = c.dram("xs1")
        self.d_xs2 = c.dram("xs2")
        self.d_oT = c.dram("oT")
        self.d_aT = c.dram("aT")
        self.d_y = c.dram("y")
        self.build()

    def consts(self):
        c = self.c
        self.identf = c.sb("identf", [128, 128], F32)
        self.ident = c.sb("ident", [128, 128], BF16)
        self.ones_b = c.sb("ones_b", [128, 128], BF16)
        self.ones_f = c.sb("ones_f", [128, 128], F32)
        self.nhalf = c.sb("nhalf", [128, 512], F32)
        self.gT = c.sb("gT", [128, 2, 5, 8], F32, dma=True)
        self.cw = c.sb("cw", [128, 2, 3, FC], F32, dma=True)
        self.cb = c.sb("cb", [128, 2, FC], F32, dma=True)
        idf, idb = self.identf, self.ident
        c.op("pool", lambda e: e.memset(idf.t[:], 0.0), writes=[idf])
        c.op("pool", lambda e: e.affine_select(out=idf.t[:], in_=idf.t[:], pattern=[[-1, 128]],
                                               compare_op=ALU.not_equal, fill=1.0, base=0, channel_multiplier=1),
             reads=[idf], writes=[idf])
        c.op("dve", lambda e: e.tensor_copy(out=idb.t[:], in_=idf.t[:]), reads=[idf], writes=[idb])
        c.op("dve", lambda e: e.memset(self.ones_b.t[:], 1.0), writes=[self.ones_b])
        c.op("dve", lambda e: e.memset(self.ones_f.t[:], 1.0), writes=[self.ones_f])
        c.op("dve", lambda e: e.memset(self.nhalf.t[:], -0.5), writes=[self.nhalf])
        c.dma("sp", self.gT.t[:], self.w["gT"], self.gT, writes=[self.gT])
        c.dma("sp", self.cw.t[:], self.w["cw"], self.cw, writes=[self.cw])
        c.dma("sp", self.cb.t[:], self.w["cb"], self.cb, writes=[self.cb])

    def rstd_from_ss(self, ss_ap, out_ap, bufs, n):
        c = self.c
        c.op("dve", lambda e: e.tensor_scalar(out=out_ap, in0=ss_ap, scalar1=1.0 / n, scalar2=EPS,
                                              op0=ALU.mult, op1=ALU.add), reads=bufs, writes=bufs)
        c.op("pool", lambda e: e.tensor_tensor(out=out_ap, in0=out_ap, in1=self.nhalf.t[:, 0:1], op=ALU.pow),
             reads=list(bufs) + [self.nhalf], writes=bufs)

    def load_weight(self, dst, dst_ap_fn, src_ap_fn, nk, ncols, stage_ring, gcol_fn=None, engs=("dve",)):
        c = self.c
        scol = stage_ring.bufs[0].shape[-1]
        skc = stage_ring.bufs[0].shape[1]
        n = 0
        for k0 in range(0, nk, skc):
            k1 = min(nk, k0 + skc)
            for c0 in range(0, ncols, scol):
                c1 = min(ncols, c0 + scol)
                st = stage_ring.next()
                c.dma("sp", st.t[:, 0:k1 - k0, 0:c1 - c0], src_ap_fn(k0, k1, c0, c1), st, writes=[st])
                for k in range(k0, k1):
                    eng = engs[n % len(engs)]
                    n += 1
                    src_ap = st.t[:, k - k0, 0:c1 - c0]
                    out_ap = dst_ap_fn(k, c0, c1)
                    if gcol_fn is not None:
                        g_ap, g_buf = gcol_fn(k)
                        if eng == "act":
                            c.op("act", lambda e, o=out_ap, i=src_ap, g=g_ap: e.activation(out=o, in_=i, func=AF.Copy, scale=g),
                                 reads=[st, g_buf], writes=[dst])
                        else:
                            c.op(eng, lambda e, o=out_ap, i=src_ap, g=g_ap: e.tensor_scalar(
                                out=o, in0=i, scalar1=g, scalar2=None, op0=ALU.mult), reads=[st, g_buf], writes=[dst])
                    else:
                        if eng == "act":
                            c.op("act", lambda e, o=out_ap, i=src_ap: e.activation(out=o, in_=i, func=AF.Copy),
                                 reads=[st], writes=[dst])
                        else:
                            c.op(eng, lambda e, o=out_ap, i=src_ap: e.tensor_copy(out=o, in_=i), reads=[st], writes=[dst])

    def pipeline(self, n, stages, skews):
        order = sorted(zip(stages, skews), key=lambda fs: -fs[1])
        for i in range(n + max(skews)):
            for fn, sk in order:
                t = i - sk
                if 0 <= t < n:
                    fn(t)

    def stats(self, src_ap, src_bufs, st, col, junk):
        c = self.c
        c.op("act", lambda e: e.activation(out=junk.t[:], in_=src_ap, func=AF.Square, accum_out=st.t[:, col:col + 1]),
             reads=list(src_bufs) + [st], writes=[st])
        self.rstd_from_ss(st.t[:, col:col + 1], st.t[:, col + 1:col + 2], [st], D)

    def norm_part(self, xt, ring_junk, ring_hb, stat, si):
        c = self.c
        junk = ring_junk.bufs[0]
        c.op("act", lambda e: e.activation(out=junk.t[:], in_=xt.t[:], func=AF.Square, accum_out=stat.t[:, si:si + 1]),
             reads=[xt], writes=[stat])
        self.rstd_from_ss(stat.t[:, si:si + 1], stat.t[:, si + 1:si + 2], [stat], D)
        hb = ring_hb.next()
        c.op("act", lambda e: e.activation(out=hb.t[:], in_=xt.t[:], func=AF.Copy, scale=stat.t[:, si + 1:si + 2]),
             reads=[xt, stat], writes=[hb])
        return hb

    def transpose_part(self, hb, hT, tcol, ring_pT, eng="dve"):
        c = self.c
        pT = ring_pT.next()
        for k in range(KC):
            c.op("pe", lambda e, k=k: e.transpose(out=pT.t[:, k, :], in_=hb.t[:, k * 128:(k + 1) * 128],
                                                  identity=self.ident.t[:]),
                 reads=[hb, self.ident], writes=[pT], inc=(k == KC - 1))
        if eng == "act":
            c.op("act", lambda e: e.activation(out=hT.t[:, :, tcol:tcol + 128], in_=pT.t[:], func=AF.Copy), reads=[pT], writes=[hT])
        else:
            c.op("dve", lambda e: e.tensor_copy(out=hT.t[:, :, tcol:tcol + 128], in_=pT.t[:]), reads=[pT], writes=[hT])

    def norm_transpose(self, xt, hT, tcol, ring_junk, ring_hb, ring_pT, stat, si):
        hb = self.norm_part(xt, ring_junk, ring_hb, stat, si)
        self.transpose_part(hb, hT, tcol, ring_pT)

    def phase1(self, L, x_src, d_src, hT):
        c = self.c
        c.push()
        xr = c.ring("p1x", 4, [128, D], F32, dma=True)
        junk = c.sb("p1j", [128, D], BF16)
        hbr = c.ring("p1h", 3, [128, D], BF16)
        pT = c.ring("p1p", 2, [128, KC, 128], BF16, psum=True)
        statr = c.ring("p1s", 6, [128, 4], F32)
        xl, sts, hbs = {}, {}, {}

        def issue(t):
            xt = xr.next()
            c.dma("sp", xt.t[:], x_src[t * 128:(t + 1) * 128, :], xt, writes=[xt])
            xl[t] = xt

        def s1(t):
            if t + 2 < NT:
                issue(t + 2)
            st = statr.next()
            c.op("dve", lambda e: e.memset(st.t[:], 0.0), writes=[st])
            self.stats(xl[t].t[:], [xl[t]], st, 0, junk)
            sts[t] = st

        def s2(t):
            xt, st = xl.pop(t), sts.pop(t)
            hb = hbr.next()
            c.op("act", lambda e: e.activation(out=hb.t[:], in_=xt.t[:], func=AF.Copy, scale=st.t[:, 1:2]),
                 reads=[xt, st], writes=[hb])
            hbs[t] = hb

        def s3(t):
            self.transpose_part(hbs.pop(t), hT, t * 128, pT)

        for t in range(2):
            issue(t)
        self.pipeline(NT, [s1, s2, s3], [0, 1, 2])
        c.pop()

    def phase2_na(self, L, hT):
        c = self.c
        c.push()
        wst = c.ring("wst", 2, [128, KC, 128], F32, dma=True)
        wbr = c.ring("wb", 2, [128, KC, 384], BF16)
        QT = c.sb("QT", [128, S], BF16)
        KT = c.sb("KT", [128, 2, S], BF16)
        c.op("pool", lambda e: e.memset(KT.t[:], 0.0), writes=[KT])
        Vaug = c.sb("Vaug", [128, 2, NT, 128], BF16)
        c.op("pool", lambda e: e.memset(Vaug.t[:], 1.0), writes=[Vaug])
        nst = c.sb("nst", [128, NA_NT, 256], F32, dma=True)
        nb = [c.sb("nb%d" % a, [128, NA_NT, 256], BF16) for a in range(2)]
        PT = c.ring("PT", 3, [128, 1536], BF16)
        OTp = c.ring("OTp", 2, [128, S], BF16, dma=True)
        rr = c.ring("rr", 2, [128, 512], F32)
        pS = c.ring("pS", 2, [128, 1536], F32, psum=True)
        pO = [c.ps("pO%d" % a, [128, 512], F32) for a in range(2)]

        def v_evac(p, tb):
            pv = p.t[:, 0:512].rearrange("p (a b) -> p a b", a=4)
            c.op("dve", lambda e: e.tensor_copy(out=Vaug.t[:, 0, tb * 4:(tb + 1) * 4, 0:64], in_=pv[:, :, 0:64]),
                 reads=[p], writes=[Vaug])
            c.op("dve", lambda e: e.tensor_copy(out=Vaug.t[:, 1, tb * 4:(tb + 1) * 4, 64:128], in_=pv[:, :, 64:128]),
                 reads=[p], writes=[Vaug])

        wb_next = self.load_qkv_w(L, 0, self.w["wqkv0"], wst, wbr)
        for hp in range(self.na_pairs):
            wb = wb_next
            self.qkv_compute(wb, hT, QT, KT, None, pS, v_evac=v_evac)
            if hp + 1 < self.na_pairs:
                wb_next = self.load_qkv_w(L, hp + 1, self.w["wqkv0"], wst, wbr)
            for a in range(2):
                h = hp * 2 + a
                c.dma("sp", nst.t[:], self.w["nab"][h], nst, writes=[nst])
                c.op("pool", lambda e, a=a: e.tensor_copy(out=nb[a].t[:], in_=nst.t[:]), reads=[nst], writes=[nb[a]])
            ot = OTp.next()
            units = [(J, a, pp) for J in range(NB) for a in range(2) for pp in range(2)]
            state = {}

            def stage_a(u):
                J, a, pp = units[u]
                P = 2 * J + pp
                chunks = na_chunks(P)
                ps = pS.next()
                n = len(chunks)
                for i, ch in enumerate(chunks):
                    tid = na_tile_id(P, ch)
                    c.op("pe", lambda e, i=i, ch=ch: e.matmul(
                        ps.t[:, i * 256:(i + 1) * 256], lhsT=KT.t[:, a, ch * 128:(ch + 1) * 128],
                        rhs=QT.t[:, P * 256:(P + 1) * 256], start=True, stop=False),
                        reads=[KT, QT], writes=[ps], inc=False)
                    c.op("pe", lambda e, i=i, tid=tid: e.matmul(
                        ps.t[:, i * 256:(i + 1) * 256], lhsT=self.ident.t[:], rhs=nb[a].t[:, tid, :],
                        start=False, stop=True), reads=[self.ident, nb[a]], writes=[ps], inc=(i == n - 1))
                pt = PT.next()
                c.op("act", lambda e: e.activation(out=pt.t[:, 0:n * 256], in_=ps.t[:, 0:n * 256], func=AF.Exp),
                     reads=[ps], writes=[pt])
                state[u] = (pt, chunks)

            def stage_b(u):
                J, a, pp = units[u]
                pt, chunks = state.pop(u)
                n = len(chunks)
                for i, ch in enumerate(chunks):
                    c.op("pe", lambda e, i=i, ch=ch: e.matmul(
                        pO[a].t[:, pp * 256:(pp + 1) * 256], lhsT=Vaug.t[:, a, ch, :], rhs=pt.t[:, i * 256:(i + 1) * 256],
                        start=(i == 0), stop=(i == n - 1)), reads=[Vaug, pt], writes=[pO[a]], inc=(i == n - 1))
                if pp == 1:
                    ol, oh = 64 * a, 64 * a + 64
                    sl, sh = 64 * (1 - a), 64 * (1 - a) + 64
                    r = rr.next()
                    c.op("dve", lambda e: e.reciprocal(out=r.t[ol:oh, :], in_=pO[a].t[sl:sh, :]),
                         reads=[pO[a]], writes=[r])
                    c.op("dve", lambda e: e.tensor_tensor(out=ot.t[ol:oh, J * 512:(J + 1) * 512], in0=pO[a].t[ol:oh, :],
                                                          in1=r.t[ol:oh, :], op=ALU.mult),
                         reads=[pO[a], r], writes=[ot])

            nu = len(units)
            stage_a(0)
            for u in range(nu):
                if u + 1 < nu:
                    stage_a(u + 1)
                stage_b(u)
            c.dma("sp", self.oT[hp], ot.t[:], ot, reads=[ot])
        c.pop()

    def load_qkv_w(self, L, hp, wq_dram, wst, wbr):
        wb = wbr.next()
        for part in range(3):
            col0 = part * D + hp * 128
            self.load_weight(wb, lambda k, c0, c1, part=part: wb.t[:, k, part * 128 + c0:part * 128 + c1],
                             lambda k0, k1, c0, c1, col0=col0: wq_dram[:, k0:k1, col0 + c0:col0 + c1],
                             KC, 128, wst, gcol_fn=lambda k: (self.gT.t[:, L, 0, k:k + 1], self.gT))
        return wb

    def qkv_compute(self, wb, hT, QT, KT, V, pS, view=None, v_evac=None):
        c = self.c
        if view is None:
            view = lambda p, a=0, b=512: p.t[a:b, 0:512] if False else p.t[:, 0:512]
        for tb in range(NB):
            for part, dst in ((0, QT), (1, KT)):
                p = pS.next()
                for k in range(KC):
                    c.op("pe", lambda e, k=k, part=part, p=p: e.matmul(
                        view(p), lhsT=wb.t[:, k, part * 128:(part + 1) * 128], rhs=hT.t[:, k, tb * 512:(tb + 1) * 512],
                        start=(k == 0), stop=(k == KC - 1)), reads=[wb, hT], writes=[p], inc=(k == KC - 1))
                if part == 0:
                    c.op("act", lambda e, p=p: e.activation(out=QT.t[:, tb * 512:(tb + 1) * 512], in_=view(p),
                                                            func=AF.Copy, scale=0.125), reads=[p], writes=[QT])
                else:
                    c.op("dve", lambda e, p=p: e.tensor_copy(out=KT.t[0:64, 0, tb * 512:(tb + 1) * 512], in_=view(p)[0:64, :]),
                         reads=[p], writes=[KT])
                    c.op("dve", lambda e, p=p: e.tensor_copy(out=KT.t[64:128, 1, tb * 512:(tb + 1) * 512], in_=view(p)[64:128, :]),
                         reads=[p], writes=[KT])
            p = pS.next()
            for tt in range(4):
                t = tb * 4 + tt
                for k in range(KC):
                    c.op("pe", lambda e, k=k, t=t, tt=tt, p=p: e.matmul(
                        view(p)[:, tt * 128:(tt + 1) * 128], lhsT=hT.t[:, k, t * 128:(t + 1) * 128], rhs=wb.t[:, k, 256:384],
                        start=(k == 0), stop=(k == KC - 1)), reads=[wb, hT], writes=[p], inc=(k == KC - 1 and tt == 3))
            if v_evac is not None:
                v_evac(p, tb)
            else:
                c.op("dve", lambda e, p=p: e.tensor_copy(out=V.t[:, tb * 4:(tb + 1) * 4, :],
                                                         in_=view(p).rearrange("p (a b) -> p a b", a=4)),
                     reads=[p], writes=[V])

    def phase2_da(self, L, hT):
        c = self.c
        c.push()
        wst = c.ring("wst", 2, [128, KC, 128], F32, dma=True)
        wbr = c.ring("wb", 2, [128, KC, 384], BF16)
        QT = c.sb("QT", [128, S], BF16)
        KT = c.sb("KT", [128, 2, S], BF16)
        c.op("pool", lambda e: e.memset(KT.t[:], 0.0), writes=[KT])
        V3 = c.sb("V3", [128, 3, NT, 128], BF16)
        eb = c.sb("eb", [128, 8, 2], F32)
        onesx = c.sb("onesx", [128, 2, 128], BF16)
        sst = c.sb("sst", [128, 1152], F32, dma=True)
        strip = c.sb("strip", [128, 1152], BF16)
        t5c = c.sb("t5c", [128, 8, 2], F32, dma=True)
        lam = c.sb("lam", [128, 4, 64], F32, dma=True)
        sg = c.sb("sg", [128, 1], F32, dma=True)
        lj = c.sb("lj", [128, 2, 64], F32)
        lv = c.sb("lv", [128, 8], F32)
        PT = c.ring("PT", DA_PT, [128, 512], BF16)
        OTh = c.ring("OTh", 2, [128, S], BF16, dma=True)
        tmp = c.ring("tmp", 10, [128, 512], F32)
        accr = c.ring("acc", 4, [128, 512], F32)
        SUM_PE_EVERY = DA_SUM_EVERY
        pS = c.ring("pS", DA_DEPTH + 1, [128, 512], F32, psum=True)
        pO = [c.ps("pO%d" % a, [128, 512], F32) for a in range(2)]
        pSum = [c.ps("pSum%d" % a, [128, 512], F32) for a in range(2)]
        c.dma("sp", t5c.t[:], self.w["t5c"], t5c, writes=[t5c])
        c.dma("sp", lam.t[:], self.w["lam"], lam, writes=[lam])
        c.dma("sp", sg.t[:], self.w["sg"], sg, writes=[sg])
        c.op("act", lambda e: e.activation(out=eb.t[:], in_=t5c.t[:], func=AF.Exp), reads=[t5c], writes=[eb])
        c.op("dve", lambda e: e.memset(lv.t[:], 0.0), writes=[lv])
        c.op("dve", lambda e: e.tensor_tensor(out=lj.t[:, 0, :], in0=lam.t[:, 0, :], in1=lam.t[:, 1, :], op=ALU.mult),
             reads=[lam], writes=[lj])
        c.op("dve", lambda e: e.tensor_tensor(out=lj.t[:, 1, :], in0=lam.t[:, 2, :], in1=lam.t[:, 3, :], op=ALU.mult),
             reads=[lam, lj], writes=[lj])
        c.op("dve", lambda e: e.reduce_sum(out=lv.t[:, 0:2], in_=lj.t[:], axis=mybir.AxisListType.X), reads=[lj], writes=[lv])
        c.op("act", lambda e: e.activation(out=lv.t[:, 2:4], in_=lv.t[:, 0:2], func=AF.Exp), reads=[lv], writes=[lv])
        c.op("dve", lambda e: e.tensor_tensor(out=lv.t[:, 4:5], in0=lv.t[:, 3:4], in1=lv.t[:, 2:3], op=ALU.subtract),
             reads=[lv], writes=[lv])
        c.op("dve", lambda e: e.tensor_scalar(out=lv.t[:, 4:5], in0=lv.t[:, 4:5], scalar1=-LAMBDA_INIT1, scalar2=None,
                                              op0=ALU.add), reads=[lv], writes=[lv])
        c.op("dve", lambda e: e.tensor_scalar(out=lv.t[:, 5:6], in0=sg.t[:], scalar1=1.0 - LAMBDA_INIT1, scalar2=None,
                                              op0=ALU.mult), reads=[lv, sg], writes=[lv])
        wb_next = self.load_qkv_w(L, 0, self.w["wqkv1"], wst, wbr)
        deferred = []

        def run_deferred(force=False):
            items = deferred[:]
            del deferred[:]
            for item in items:
                item[0] -= 1
                if item[0] <= 0 or force:
                    item[1]()
                else:
                    deferred.append(item)

        for h in range(self.da_heads):
            wb = wb_next

            def v_evac(p, tb, h=h):
                pv = p.t[:, 0:512].rearrange("p (a b) -> p a b", a=4)
                c.op("dve", lambda e: e.tensor_copy(out=V3.t[:, 0, tb * 4:(tb + 1) * 4, :], in_=pv), reads=[p], writes=[V3])
                for side in range(2):
                    c.op("act", lambda e, side=side: e.activation(out=V3.t[:, 1 + side, tb * 4:(tb + 1) * 4, :], in_=pv,
                                                                  func=AF.Copy, scale=eb.t[:, h, side:side + 1]),
                         reads=[p, eb], writes=[V3])
            self.qkv_compute(wb, hT, QT, KT, None, pS, v_evac=v_evac)
            for side in range(2):
                c.op("dve", lambda e, side=side: e.tensor_scalar(out=onesx.t[:, side, :], in0=self.ones_b.t[:],
                                                                 scalar1=eb.t[:, h, side:side + 1], scalar2=None, op0=ALU.mult),
                     reads=[self.ones_b, eb], writes=[onesx])
            if h + 1 < 8:
                wb_next = self.load_qkv_w(L, h + 1, self.w["wqkv1"], wst, wbr)
            c.dma("sp", sst.t[:], self.w["t5s"][h], sst, writes=[sst])
            c.op("pool", lambda e: e.tensor_copy(out=strip.t[:], in_=sst.t[:]), reads=[sst], writes=[strip])
            ot = OTh.next()
            steps = [(J, m, kc) for J in range(NB) for m in range(2) for kc in range(NT)]
            state = {}
            tts = {}
            accs = {}

            def stage_a(s):
                J, m, kc = steps[s]
                pl, ph = 64 * m, 64 * m + 64
                off = kc * 128 - J * 512
                near = -256 < off < 640
                ps = pS.next()
                c.op("pe", lambda e: e.matmul(
                    ps.t[:], lhsT=KT.t[:, m, kc * 128:(kc + 1) * 128], rhs=QT.t[:, J * 512:(J + 1) * 512],
                    start=True, stop=not near), reads=[KT, QT], writes=[ps], inc=not near)
                if near:
                    u0 = 512 - off
                    c.op("pe", lambda e: e.matmul(
                        ps.t[:], lhsT=self.ident.t[:], rhs=strip.t[:, u0:u0 + 512], start=False, stop=True),
                        reads=[self.ident, strip], writes=[ps])
                pt = PT.next()
                c.op("act", lambda e: e.activation(out=pt.t[:], in_=ps.t[:], func=AF.Exp), reads=[ps], writes=[pt])
                state[s] = (pt, 0 if near else (1 if off < 0 else 2))

            def post_m(J, m):
                r = tmp.next()
                c.op("dve", lambda e: e.reciprocal(out=r.t[:], in_=pSum[m].t[:]), reads=[pSum[m]], writes=[r])
                t_ = tmp.next()
                c.op("dve", lambda e: e.tensor_tensor(out=t_.t[:], in0=pO[m].t[:], in1=r.t[:], op=ALU.mult),
                     reads=[pO[m], r], writes=[t_])
                tts[(J, m)] = t_

            def post_j(J, ot):
                t0_, t1_ = tts.pop((J, 0)), tts.pop((J, 1))
                o = tmp.next()
                c.op("dve", lambda e: e.scalar_tensor_tensor(out=o.t[:], in0=t1_.t[:], scalar=lv.t[:, 4:5], in1=t0_.t[:],
                                                             op0=ALU.mult, op1=ALU.add), reads=[t0_, t1_, lv], writes=[o])
                sq = tmp.next()
                c.op("pool", lambda e: e.tensor_tensor(out=sq.t[:], in0=o.t[:], in1=o.t[:], op=ALU.mult),
                     reads=[o], writes=[sq])

                def fin():
                    pst = pS.next()
                    c.op("pe", lambda e: e.matmul(pst.t[:], lhsT=self.ones_f.t[:], rhs=sq.t[:], start=True, stop=True),
                         reads=[self.ones_f, sq], writes=[pst])
                    c.op("dve", lambda e: e.tensor_scalar(out=sq.t[:], in0=pst.t[:], scalar1=1.0 / 128, scalar2=EPS,
                                                          op0=ALU.mult, op1=ALU.add), reads=[pst], writes=[sq])
                    c.op("act", lambda e: e.activation(out=sq.t[:], in_=sq.t[:], func=AF.Ln), reads=[sq], writes=[sq])
                    c.op("act", lambda e: e.activation(out=sq.t[:], in_=sq.t[:], func=AF.Exp, scale=-0.5), reads=[sq], writes=[sq])
                    c.op("dve", lambda e: e.scalar_tensor_tensor(
                        out=ot.t[:, J * 512:(J + 1) * 512], in0=o.t[:], scalar=lv.t[:, 5:6], in1=sq.t[:],
                        op0=ALU.mult, op1=ALU.mult), reads=[o, sq, lv], writes=[ot])
                deferred.append([6, fin])

            def stage_b(s, ot, h=h):
                J, m, kc = steps[s]
                pt, side = state.pop(s)
                c.op("pe", lambda e: e.matmul(
                    pO[m].t[:], lhsT=V3.t[:, side, kc, :], rhs=pt.t[:], start=(kc == 0), stop=(kc == NT - 1)),
                    reads=[V3, pt], writes=[pO[m]], inc=(kc % SUM_PE_EVERY != 0))
                if kc % SUM_PE_EVERY == 0:
                    lw = self.ones_b.t[:] if side == 0 else onesx.t[:, side - 1, :]
                    c.op("pe", lambda e: e.matmul(
                        pSum[m].t[:], lhsT=lw, rhs=pt.t[:], start=(kc == 0), stop=False),
                        reads=[self.ones_b, onesx, pt], writes=[pSum[m]])
                else:
                    first = (J, m) not in accs
                    if DA_SKIP_DVE and not first:
                        return_early = True
                    else:
                        return_early = False
                    if first:
                        accs[(J, m)] = accr.next()
                    ac = accs[(J, m)]
                    if first and side == 0:
                        c.op("dve", lambda e: e.tensor_copy(out=ac.t[:], in_=pt.t[:]), reads=[pt], writes=[ac])
                    elif first:
                        c.op("dve", lambda e: e.tensor_scalar(out=ac.t[:], in0=pt.t[:], scalar1=eb.t[:, h, side - 1:side],
                                                              scalar2=None, op0=ALU.mult), reads=[pt, eb], writes=[ac])
                    elif return_early:
                        pass
                    elif side == 0:
                        c.op("dve", lambda e: e.tensor_tensor(out=ac.t[:], in0=ac.t[:], in1=pt.t[:], op=ALU.add),
                             reads=[pt, ac], writes=[ac])
                    else:
                        c.op("dve", lambda e: e.scalar_tensor_tensor(out=ac.t[:], in0=pt.t[:], scalar=eb.t[:, h, side - 1:side],
                                                                     in1=ac.t[:], op0=ALU.mult, op1=ALU.add),
                             reads=[pt, ac, eb], writes=[ac])
                if kc == NT - 1:
                    def fin_m(J=J, m=m):
                        ac = accs.pop((J, m))
                        c.op("pe", lambda e: e.matmul(pSum[m].t[:], lhsT=self.ones_f.t[:], rhs=ac.t[:], start=False, stop=True),
                             reads=[self.ones_f, ac], writes=[pSum[m]])
                        post_m(J, m)
                        if m == 1:
                            post_j(J, ot)
                    deferred.append([3, fin_m])

            ns = len(steps)
            DEPTH = DA_DEPTH
            for s in range(min(DEPTH, ns)):
                stage_a(s)
            for s in range(ns):
                if s + DEPTH < ns:
                    stage_a(s + DEPTH)
                stage_b(s, ot)
                run_deferred()
            while deferred:
                run_deferred(force=True)
            c.dma("sp", self.oT[h], ot.t[:], ot, reads=[ot])
        c.pop()

    def phase3(self, L, x_src, d_src, hT):
        c = self.c
        c.push()
        wo = c.sb("wo", [128, KC, D], BF16)
        g1 = c.sb("g1", [128, D], F32, dma=True)
        junk = c.sb("junk", [128, D], BF16)
        wsrc = self.w["wo0"] if L == 0 else self.w["wo1"]
        c.dma("sp", g1.t[:], self.w["gB"][L, 0, :].partition_broadcast(128), g1, writes=[g1])
        c.push()
        wst = c.ring("wst", 2, [128, KC, 512], F32, dma=True)
        self.load_weight(wo, lambda k, c0, c1: wo.t[:, k, c0:c1], lambda k0, k1, c0, c1: wsrc[:, k0:k1, c0:c1],
                         KC, D, wst, engs=("dve", "act"))
        c.pop()
        OTb = c.ring("OTb", 2, [128, 8, 512], BF16, dma=True)
        xr = c.ring("xr", 5, [128, D], F32, dma=True)
        tmp = c.ring("tmp", 2, [128, D], F32)
        x1r = c.ring("x1", 4, [128, D], F32, dma=True)
        hbr = c.ring("hb", 3, [128, D], BF16)
        statr = c.ring("stat", 8, [128, 4], F32)
        pm = c.ring("pm", 3, [128, 2, 512], F32, psum=True)
        pT = c.ring("pT", 2, [128, KC, 128], BF16, psum=True)
        xl, obl, ps_, sts, x1s, hbs = {}, {}, {}, {}, {}, {}

        def issue(t):
            xt = xr.next()
            c.dma("sp", xt.t[:], x_src[t * 128:(t + 1) * 128, :], xt, writes=[xt])
            xl[t] = xt

        def issue_b(tb):
            ob = OTb.next()
            c.dma("sp", ob.t[:], self.oT[:, :, tb * 512:(tb + 1) * 512].rearrange("h p t -> p h t"), ob,
                  writes=[ob])
            obl[tb] = ob

        def s1a(t):
            tb, tt = t // 4, t % 4
            if tt == 0 and tb + 1 < NB:
                issue_b(tb + 1)
            if t + 3 < NT:
                issue(t + 3)
            ob = obl[tb]
            p = pm.next()
            for half in range(2):
                for k in range(KC):
                    c.op("pe", lambda e, k=k, half=half: e.matmul(
                        p.t[:, half, :], lhsT=ob.t[:, k, tt * 128:(tt + 1) * 128], rhs=wo.t[:, k, half * 512:(half + 1) * 512],
                        start=(k == 0), stop=(k == KC - 1)), reads=[ob, wo], writes=[p], inc=(k == KC - 1 and half == 1))
            ps_[t] = p

        def s1b(t):
            p = ps_[t]
            st = statr.next()
            c.op("dve", lambda e: e.memset(st.t[:], 0.0), writes=[st])
            self.stats(p.t[:].rearrange("p a b -> p (a b)"), [p], st, 0, junk)
            sts[t] = st

        def s2a(t):
            p, st, xt = ps_.pop(t), sts[t], xl.pop(t)
            tm = tmp.next()
            c.op("dve", lambda e: e.scalar_tensor_tensor(out=tm.t[:], in0=p.t[:].rearrange("p a b -> p (a b)"),
                                                         scalar=st.t[:, 1:2], in1=g1.t[:],
                                                         op0=ALU.mult, op1=ALU.mult), reads=[p, st, g1], writes=[tm])
            x1 = x1r.next()
            c.op("pool", lambda e: e.tensor_tensor(out=x1.t[:], in0=tm.t[:], in1=xt.t[:], op=ALU.add),
                 reads=[tm, xt], writes=[x1])
            c.dma("sp", self.xs1[t * 128:(t + 1) * 128, :], x1.t[:], x1, reads=[x1])
            x1s[t] = x1

        def s2b(t):
            self.stats(x1s[t].t[:], [x1s[t]], sts[t], 2, junk)

        def s3(t):
            x1, st = x1s.pop(t), sts.pop(t)
            hb = hbr.next()
            c.op("act", lambda e: e.activation(out=hb.t[:], in_=x1.t[:], func=AF.Copy, scale=st.t[:, 3:4]),
                 reads=[x1, st], writes=[hb])
            hbs[t] = hb

        def s4(t):
            self.transpose_part(hbs.pop(t), hT, t * 128, pT)

        issue_b(0)
        for t in range(3):
            issue(t)
        self.pipeline(NT, [s1a, s1b, s2a, s2b, s3, s4], [0, 1, 2, 3, 4, 5])
        c.pop()

    def post_norm_residual(self, p, xt, gB, xo, tmp_ring, junk_ring, stat, si):
        c = self.c
        junk = junk_ring.bufs[0]
        c.op("act", lambda e: e.activation(out=junk.t[:], in_=p.t[:].rearrange("p a b -> p (a b)"), func=AF.Square,
                                           accum_out=stat.t[:, si:si + 1]), reads=[p], writes=[stat])
        self.rstd_from_ss(stat.t[:, si:si + 1], stat.t[:, si + 1:si + 2], [stat], D)
        tm = tmp_ring.next()
        c.op("dve", lambda e: e.scalar_tensor_tensor(out=tm.t[:], in0=p.t[:].rearrange("p a b -> p (a b)"),
                                                     scalar=stat.t[:, si + 1:si + 2], in1=gB.t[:],
                                                     op0=ALU.mult, op1=ALU.mult), reads=[p, stat, gB], writes=[tm])
        c.op("pool", lambda e: e.tensor_tensor(out=xo.t[:], in0=tm.t[:], in1=xt.t[:], op=ALU.add),
             reads=[tm, xt], writes=[xo])

    def phase4a(self, L, hT):
        c = self.c
        c.push()
        wst = c.ring("wst", 2, [128, KC, 384], F32, dma=True)
        wir = c.ring("wi", 2, [128, KC, 2 * 768], BF16)
        gsb = c.ring("gsb", 2, [128, 514], F32)
        t1 = c.ring("t1", 2, [128, 512], F32)
        t2 = c.ring("t2", 2, [128, 512], F32)
        ge = c.ring("ge", 2, [128, 512], F32)
        aTb = c.ring("aTb", 2, [128, 6, 512], BF16, dma=True)
        pg = c.ring("pg", 2, [128, 512], F32, psum=True)
        pv = c.ring("pv", 2, [128, 512], F32, psum=True)
        win = self.w["win"][L]
        def load_group(gi):
            fc0, nf = FGROUPS[gi]
            wi = wir.next()
            for part in range(2):
                col0 = part * FF + fc0 * 128
                self.load_weight(wi, lambda k, c0, c1, part=part: wi.t[:, k, part * 768 + c0:part * 768 + c1],
                                 lambda k0, k1, c0, c1, col0=col0: win[:, k0:k1, col0 + c0:col0 + c1],
                                 KC, nf * 128, wst, gcol_fn=lambda k: (self.gT.t[:, L, 2, k:k + 1], self.gT), engs=("act",))
            return wi
        BW = 510
        blocks = [(s0, min(BW, S - s0)) for s0 in range(0, S, BW)]
        wi_next = load_group(0)
        for gi, (fc0, nf) in enumerate(FGROUPS):
            wi = wi_next
            if gi + 1 < len(FGROUPS):
                wi_next = load_group(gi + 1)
            for (s0, W) in blocks:
                glo, ghi = max(s0 - 1, 0), min(s0 + W + 1, S)
                gn = ghi - glo
                goff = glo - (s0 - 1)
                ab = aTb.next()
                for fi in range(nf):
                    fc = fc0 + fi
                    g_ps = pg.next()
                    v_ps = pv.next()
                    for k in range(KC):
                        c.op("pe", lambda e, k=k: e.matmul(
                            g_ps.t[:, 0:gn], lhsT=wi.t[:, k, fi * 128:(fi + 1) * 128], rhs=hT.t[:, k, glo:ghi],
                            start=(k == 0), stop=(k == KC - 1)), reads=[wi, hT], writes=[g_ps], inc=(k == KC - 1))
                    for k in range(KC):
                        c.op("pe", lambda e, k=k: e.matmul(
                            v_ps.t[:, 0:W], lhsT=wi.t[:, k, 768 + fi * 128:768 + (fi + 1) * 128], rhs=hT.t[:, k, s0:s0 + W],
                            start=(k == 0), stop=(k == KC - 1)), reads=[wi, hT], writes=[v_ps], inc=(k == KC - 1))
                    gs = gsb.next()
                    c.op("act", lambda e: e.activation(out=gs.t[:, goff:goff + gn], in_=g_ps.t[:, 0:gn], func=AF.Copy),
                         reads=[g_ps], writes=[gs])
                    if goff == 1:
                        c.op("dve", lambda e: e.memset(gs.t[:, 0:1], 0.0), writes=[gs])
                    if goff + gn < W + 2:
                        c.op("dve", lambda e: e.memset(gs.t[:, W + 1:W + 2], 0.0), writes=[gs])
                    a1 = t1.next()
                    a2 = t2.next()
                    cwv = lambda j_: self.cw.t[:, L, j_, fc:fc + 1]
                    c.op("dve", lambda e: e.tensor_scalar(out=a1.t[:, 0:W], in0=gs.t[:, 0:W], scalar1=cwv(0),
                                                          scalar2=None, op0=ALU.mult),
                         reads=[gs, self.cw], writes=[a1])
                    c.op("dve", lambda e: e.scalar_tensor_tensor(
                        out=a2.t[:, 0:W], in0=gs.t[:, 1:W + 1], scalar=cwv(1), in1=a1.t[:, 0:W], op0=ALU.mult, op1=ALU.add),
                        reads=[gs, a1, self.cw], writes=[a2])
                    c.op("dve", lambda e: e.scalar_tensor_tensor(
                        out=a1.t[:, 0:W], in0=gs.t[:, 2:W + 2], scalar=cwv(2), in1=a2.t[:, 0:W], op0=ALU.mult, op1=ALU.add),
                        reads=[gs, a2, self.cw], writes=[a1])
                    gg = ge.next()
                    c.op("act", lambda e: e.activation(out=gg.t[:, 0:W], in_=a1.t[:, 0:W], func=AF.Gelu_apprx_tanh,
                                                       bias=self.cb.t[:, L, fc:fc + 1]),
                         reads=[a1, self.cb], writes=[gg])
                    c.op("dve", lambda e: e.tensor_tensor(out=ab.t[:, fi, 0:W], in0=v_ps.t[:, 0:W], in1=gg.t[:, 0:W], op=ALU.mult),
                         reads=[gg, v_ps], writes=[ab])
                c.dma("sp", self.aT[fc0:fc0 + nf, :, s0:s0 + W].rearrange("f p t -> p f t"), ab.t[:, 0:nf, 0:W], ab,
                      reads=[ab])
        c.pop()

    def phase4b(self, L, y_dst, d_dst):
        c = self.c
        c.push()
        wout = c.sb("wout", [128, FC, D], BF16)
        wg = c.sb("wg", [128, KC, D], BF16)
        wp = c.sb("wp", [128, 2, D], BF16)
        g3 = c.sb("g3", [128, D], F32, dma=True)
        junk = c.sb("junk", [128, D], BF16)
        c.dma("sp", g3.t[:], self.w["gB"][L, 1, :].partition_broadcast(128), g3, writes=[g3])
        wo_src, wg_src, wp_src = self.w["wout"][L], self.w["wg"][L], self.w["wp"][L]
        c.push()
        wst = c.ring("wst", 2, [128, 2, 1024], F32, dma=True)
        self.load_weight(wout, lambda k, c0, c1: wout.t[:, k, c0:c1], lambda k0, k1, c0, c1: wo_src[:, k0:k1, c0:c1],
                         FC, D, wst, engs=("dve", "act"))
        self.load_weight(wg, lambda k, c0, c1: wg.t[:, k, c0:c1], lambda k0, k1, c0, c1: wg_src[:, k0:k1, c0:c1],
                         KC, D, wst, gcol_fn=lambda k: (self.gT.t[:, L, 4, k:k + 1], self.gT), engs=("dve", "act"))
        self.load_weight(wp, lambda k, c0, c1: wp.t[:, k, c0:c1], lambda k0, k1, c0, c1: wp_src[:, k0:k1, c0:c1],
                         2, D, wst, engs=("dve", "act"))
        c.pop()
        aTb = c.ring("aTb", 2, [128, FC, 512], BF16, dma=True)
        pst = c.ring("pst", 2, [128, 2, 512], F32, dma=True)
        pb = c.ring("pb", 4, [128, 2, 512], BF16)
        xr = c.ring("xr", 4, [128, D], F32, dma=True)
        tmp = c.ring("tmp", 2, [128, D], F32)
        x2r = c.ring("x2", 5, [128, D], F32)
        x3r = c.ring("x3", 2, [128, D], F32, dma=True)
        sgr = c.ring("sig", 2, [128, D], F32)
        hbr = c.ring("hb", 3, [128, D], BF16)
        h3T = c.ring("h3T", 2, [128, KC, 128], BF16)
        statr = c.ring("stat", 8, [128, 4], F32)
        pf = c.ring("pf", 2, [128, 2, 512], F32, psum=True)
        pT = c.ring("pT", 1, [128, KC, 128], BF16, psum=True)
        pgp = c.ring("pgp", 3, [128, 512], F32, psum=True)
        xl, bl, ps_, sts, x2s, hbs, h3s = {}, {}, {}, {}, {}, {}, {}

        def issue(t):
            xt = xr.next()
            c.dma("sp", xt.t[:], self.xs1[t * 128:(t + 1) * 128, :], xt, writes=[xt])
            xl[t] = xt

        def issue_b(tb):
            t0 = tb * 512
            ab = aTb.next()
            for f0 in range(0, FC, 8):
                f1 = min(FC, f0 + 8)
                c.dma("sp", ab.t[:, f0:f1, :], self.aT[f0:f1, :, t0:t0 + 512].rearrange("f p t -> p f t"), ab,
                      writes=[ab])
            pq = pst.next()
            c.dma("sp", pq.t[:], self.pT[L, :, :, t0:t0 + 512], pq, writes=[pq])
            pbb = pb.next()
            c.op("pool", lambda e: e.tensor_copy(out=pbb.t[:], in_=pq.t[:]), reads=[pq], writes=[pbb])
            bl[tb] = (ab, pbb)

        def s1a(t):
            tb, tt = t // 4, t % 4
            if tt == 0 and tb + 1 < NB:
                issue_b(tb + 1)
            if t + 2 < NT:
                issue(t + 2)
            ab, pbb = bl[tb]
            p = pf.next()
            for half in range(2):
                for k in range(FC):
                    c.op("pe", lambda e, k=k, half=half: e.matmul(
                        p.t[:, half, :], lhsT=ab.t[:, k, tt * 128:(tt + 1) * 128], rhs=wout.t[:, k, half * 512:(half + 1) * 512],
                        start=(k == 0), stop=(k == FC - 1)), reads=[ab, wout], writes=[p], inc=(k == FC - 1 and half == 1))
            ps_[t] = p

        def s1b(t):
            p = ps_[t]
            st = statr.next()
            c.op("dve", lambda e: e.memset(st.t[:], 0.0), writes=[st])
            self.stats(p.t[:].rearrange("p a b -> p (a b)"), [p], st, 0, junk)
            sts[t] = st

        def s2a(t):
            p, st, xt = ps_.pop(t), sts[t], xl.pop(t)
            tm = tmp.next()
            c.op("dve", lambda e: e.scalar_tensor_tensor(out=tm.t[:], in0=p.t[:].rearrange("p a b -> p (a b)"),
                                                         scalar=st.t[:, 1:2], in1=g3.t[:],
                                                         op0=ALU.mult, op1=ALU.mult), reads=[p, st, g3], writes=[tm])
            x2 = x2r.next()
            c.op("pool", lambda e: e.tensor_tensor(out=x2.t[:], in0=tm.t[:], in1=xt.t[:], op=ALU.add),
                 reads=[tm, xt], writes=[x2])
            x2s[t] = x2

        def s2b(t):
            self.stats(x2s[t].t[:], [x2s[t]], sts[t], 2, junk)

        def s3(t):
            x2, st = x2s[t], sts.pop(t)
            hb = hbr.next()
            c.op("act", lambda e: e.activation(out=hb.t[:], in_=x2.t[:], func=AF.Copy, scale=st.t[:, 3:4]),
                 reads=[x2, st], writes=[hb])
            hbs[t] = hb

        def s4(t):
            h3 = h3T.next()
            self.transpose_part(hbs.pop(t), h3, 0, pT, eng="act")
            h3s[t] = h3

        def s5(t):
            tb, tt = t // 4, t % 4
            x2, h3 = x2s.pop(t), h3s.pop(t)
            pbb = bl[tb][1]
            x3 = x3r.next()
            for half in range(2):
                pgate = pgp.next()
                for k in range(KC):
                    c.op("pe", lambda e, k=k: e.matmul(
                        pgate.t[:], lhsT=h3.t[:, k, :], rhs=wg.t[:, k, half * 512:(half + 1) * 512],
                        start=(k == 0), stop=(k == KC - 1)), reads=[h3, wg], writes=[pgate], inc=(k == KC - 1))
                sg_ = sgr.next()
                c.op("act", lambda e: e.activation(out=sg_.t[:, 0:512], in_=pgate.t[:], func=AF.Sigmoid),
                     reads=[pgate], writes=[sg_])
                pproj = pgp.next()
                for k in range(2):
                    c.op("pe", lambda e, k=k: e.matmul(
                        pproj.t[:], lhsT=pbb.t[:, k, tt * 128:(tt + 1) * 128], rhs=wp.t[:, k, half * 512:(half + 1) * 512],
                        start=(k == 0), stop=(k == 1)), reads=[pbb, wp], writes=[pproj], inc=(k == 1))
                c.op("dve", lambda e: e.tensor_tensor(out=sg_.t[:, 512:1024], in0=pproj.t[:], in1=sg_.t[:, 0:512], op=ALU.mult),
                     reads=[pproj, sg_], writes=[sg_])
                c.op("pool", lambda e: e.tensor_tensor(
                    out=x3.t[:, half * 512:(half + 1) * 512], in0=sg_.t[:, 512:1024], in1=x2.t[:, half * 512:(half + 1) * 512],
                    op=ALU.add), reads=[sg_, x2], writes=[x3])
            c.dma("sp", y_dst[t * 128:(t + 1) * 128, :], x3.t[:], x3, reads=[x3])

        issue_b(0)
        for t in range(2):
            issue(t)
        self.pipeline(NT, [s1a, s1b, s2a, s2b, s3, s4, s5], [0, 1, 2, 3, 4, 5, 6])
        c.pop()

    def build(self):
        c = self.c
        c.push()
        self.consts()
        srcs = [(self.x, None), (self.xs2, self.d_xs2)]
        dsts = [(self.xs2, self.d_xs2), (self.y, self.d_y)]
        for L in self.layers:
            x_src, d_src = srcs[L]
            if len(self.layers) == 1:
                x_src = self.x
            c.push()
            hT = c.sb("hT", [128, KC, S], BF16)
            self.phase1(L, x_src, d_src, hT)
            if self.stop_after == (L, 1):
                self.dump_hT(hT)
                c.pop(); break
            if L == 0:
                self.phase2_na(L, hT)
            else:
                self.phase2_da(L, hT)
            if self.stop_after == (L, 2):
                c.pop(); break
            self.phase3(L, x_src, d_src, hT)
            if self.stop_after == (L, 3):
                c.pop(); break
            self.phase4a(L, hT)
            c.pop()
            if self.stop_after == (L, 4):
                break
            y_dst, d_dst = dsts[L]
            self.phase4b(L, y_dst, d_dst)
            if self.stop_after == (L, 5):
                break
        c.pop()

    def dump_hT(self, hT):
        pass


def build_program(debug=False, stop_after=None, **kw):
    return Prog(debug=debug, stop_after=stop_after, **kw)


def make_in_maps(inputs):
    inp = {k: np.asarray(v, dtype=np.float32) for k, v in inputs.items()}
    common = prep_common(inp)
    in_maps = []
    for b in range(8):
        m = dict(common)
        m["x"] = np.ascontiguousarray(inp["x"][b])
        p = inp["p"][:, b]
        m["pT"] = np.ascontiguousarray(p.reshape(2, S, 2, 128).transpose(0, 3, 2, 1))
        in_maps.append(m)
    return in_maps


def kernel(**inputs):
    in_maps = make_in_maps(inputs)
    prog = build_program()
    res = run_bass_kernel_spmd(prog.nc, in_maps, core_ids=list(range(8)))
    out = np.stack([np.asarray(r["y"], dtype=np.float32).reshape(S, D) for r in res.results], axis=0)
    return out
```

```python
import math
import numpy as np
import concourse.bass as bass
import concourse.mybir as mybir
from concourse.bass_utils import run_bass_kernel_spmd

F32 = mybir.dt.float32
BF16 = mybir.dt.bfloat16
AF = mybir.ActivationFunctionType
ALU = mybir.AluOpType

S = 4096
D = 1024
NT = 32
NB = 8
KC = 8
FF = 2816
FC = 22
EPS = 1e-6
NEG = -30000.0
LAMBDA_INIT1 = 0.8 - 0.6 * math.exp(-0.3 * 1)
FGROUPS = [(0, 6), (6, 6), (12, 5), (17, 5)]
DA_DEPTH = 3
DA_SUM_EVERY = 2
DA_PT = 8
DA_SKIP_DVE = False


class DSem:
    def __init__(self, h):
        self.h = h
        self.count = 0


class Buf:
    def __init__(self, name, t=None):
        self.name = name
        self.t = t
        self.last_w = None
        self.readers = []
        self.dsem = None


class Ctx:
    def __init__(self, nc, n_dsems=40):
        self.nc = nc
        self.engs = {"pe": nc.tensor, "act": nc.scalar, "dve": nc.vector, "pool": nc.gpsimd, "sp": nc.sync}
        self.tl = {k: DSem(nc.alloc_semaphore(name="tl_" + k)) for k in self.engs}
        self.seen = {k: {} for k in self.engs}
        self.free_dsems = [DSem(nc.alloc_semaphore(name="d%d" % i)) for i in range(n_dsems)]
        self.all_dsems = list(self.free_dsems)
        self.live = []
        self.ninst = {k: 0 for k in self.engs}

    def push(self):
        self.live.append([])

    def pop(self):
        self.barrier()
        for cm, b in reversed(self.live.pop()):
            if b.dsem is not None:
                self.free_dsems.append(b.dsem)
            if cm is not None:
                cm.__exit__(None, None, None)

    def sb(self, name, shape, dt, dma=False):
        self.uid = getattr(self, "uid", 0) + 1
        name = "s%d_%s" % (self.uid, name)
        cm = self.nc.sbuf_tensor(name, list(shape), dt)
        b = Buf(name, cm.__enter__())
        b.shape = list(shape)
        if dma:
            b.dsem = self.free_dsems.pop()
        self.live[-1].append((cm, b))
        return b

    def ps(self, name, shape, dt=F32):
        self.uid = getattr(self, "uid", 0) + 1
        name = "p%d_%s" % (self.uid, name)
        cm = self.nc.psum_tensor(name, list(shape), dt)
        b = Buf(name, cm.__enter__())
        self.live[-1].append((cm, b))
        return b

    def dram(self, name):
        b = Buf(name)
        return b

    def ring(self, name, n, shape, dt, dma=False, psum=False):
        if psum:
            return Ring([self.ps("%s%d" % (name, i), shape, dt) for i in range(n)])
        return Ring([self.sb("%s%d" % (name, i), shape, dt, dma=dma) for i in range(n)])

    def _waits(self, eng, reads, writes):
        need = {}

        def add(tok):
            if tok is None:
                return
            s, v = tok
            if eng == "pe" and s is self.tl["pe"]:
                return
            if need.get(s, 0) < v:
                need[s] = v

        for b in reads:
            add(b.last_w)
        for b in writes:
            add(b.last_w)
            for r in b.readers:
                add(r)
        e = self.engs[eng]
        seen = self.seen[eng]
        for s, v in need.items():
            if seen.get(s, 0) < v:
                e.wait_ge(s.h, v)
                seen[s] = v

    def _record(self, tok, reads, writes):
        for b in reads:
            b.readers.append(tok)
            if len(b.readers) > 16:
                m = {}
                for s, v in b.readers:
                    if m.get(s, 0) < v:
                        m[s] = v
                b.readers = list(m.items())
        for b in writes:
            b.last_w = tok
            b.readers = []

    def op(self, eng, fn, reads=(), writes=(), inc=True):
        self._waits(eng, reads, writes)
        ins = fn(self.engs[eng])
        tl = self.tl[eng]
        if inc:
            tl.count += 1
            ins.then_inc(tl.h, 1)
            tok = (tl, tl.count)
        else:
            tok = (tl, tl.count + 1)
        self.ninst[eng] += 1
        self._record(tok, reads, writes)
        return ins

    def dma(self, q, out_ap, in_ap, owner, reads=(), writes=(), **kw):
        self._waits(q, reads, writes)
        ins = self.engs[q].dma_start(out=out_ap, in_=in_ap, **kw)
        ds = owner.dsem
        ds.count += 16
        ins.then_inc(ds.h, 16)
        tok = (ds, ds.count)
        self.ninst[q] += 1
        self._record(tok, reads, writes)
        return ins

    def barrier(self):
        sems = [self.tl[k] for k in ("pe", "act", "dve", "pool")] + self.all_dsems
        for k in self.engs:
            e = self.engs[k]
            seen = self.seen[k]
            for s in sems:
                if s.count > 0 and seen.get(s, 0) < s.count:
                    e.wait_ge(s.h, s.count)
                    seen[s] = s.count


class Ring:
    def __init__(self, bufs):
        self.bufs = bufs
        self.i = 0

    def next(self):
        b = self.bufs[self.i % len(self.bufs)]
        self.i += 1
        return b


def na_tile_list():
    tiles = [(7, c) for c in range(12, 18)]
    tiles += [(0, c) for c in range(4)] + [(15, c) for c in range(28, 32)]
    return tiles


def na_chunks(P):
    lo_row = min(max(4 * P - 4, 0), 56)
    hi_row = min(max(4 * P - 1, 0), 56) + 7
    return list(range(lo_row // 2, hi_row // 2 + 1))


def na_tile_id(P, c):
    if 1 <= P <= 14:
        return c - 2 * P + 2
    return 6 + c if P == 0 else 10 + (c - 28)


NA_NT = 14


def na_index_table():
    tiles = na_tile_list()
    idx = np.zeros((len(tiles), 128, 256), np.int64)
    kk = np.arange(128)[:, None]
    qq = np.arange(256)[None, :]
    for t, (P, c) in enumerate(tiles):
        krow = 2 * c + kk // 64
        kcol = kk % 64
        qrow = 4 * P + qq // 64
        qcol = qq % 64
        r0 = np.clip(qrow - 4, 0, 56)
        c0 = np.clip(qcol - 8, 0, 48)
        valid = (krow >= r0) & (krow < r0 + 8) & (kcol >= c0) & (kcol < c0 + 16)
        ro = krow - qrow + 7
        co = kcol - qcol + 15
        idx[t] = np.where(valid, ro * 31 + co, 15 * 31)
    return idx


def t5_bucket_np(rel):
    nn = np.abs(rel)
    nf = np.maximum(nn, 1).astype(np.float32)
    lg = (np.log(nf / np.float32(8)) / np.float32(math.log(16)) * np.float32(8)).astype(np.int32) + 8
    lg = np.minimum(lg, 15)
    return np.where(rel > 0, 16, 0) + np.where(nn < 8, nn, lg)


def ktile(w):
    K, N = w.shape
    return np.ascontiguousarray(w.reshape(K // 128, 128, N).transpose(1, 0, 2))


def prep_common(inp):
    f = np.float32
    c = {}
    g = inp["norm_g"]
    c["gT"] = np.ascontiguousarray(g.reshape(2, 5, 8, 128).transpose(3, 0, 1, 2)).astype(f)
    c["gB"] = np.ascontiguousarray(g[:, [1, 3], :]).astype(f)
    c["wqkv0"] = ktile(inp["na_w_qkv"][0])
    c["wqkv1"] = ktile(inp["da_w_qkv"][0])
    c["wo0"] = ktile(inp["na_w_o"][0])
    c["wo1"] = ktile(inp["da_w_o"][0])
    c["win"] = np.stack([ktile(inp["ffn_w_in"][l]) for l in range(2)])
    c["wout"] = np.stack([ktile(inp["ffn_w_out"][l]) for l in range(2)])
    c["wg"] = np.stack([ktile(inp["ple_w_gate"][l]) for l in range(2)])
    c["wp"] = np.stack([ktile(inp["ple_w_proj"][l]) for l in range(2)])
    cw = inp["ffn_conv_w"]
    c["cw"] = np.ascontiguousarray(cw.reshape(2, 3, FC, 128).transpose(3, 0, 1, 2)).astype(f)
    c["cb"] = np.ascontiguousarray(inp["ffn_conv_b"].reshape(2, FC, 128).transpose(2, 0, 1)).astype(f)
    rpb = inp["na_rpb"][0].reshape(16, 15 * 31)
    rpb_ext = np.concatenate([rpb, np.full((16, 1), NEG, f)], axis=1)
    idx = na_index_table()
    nab = rpb_ext[:, idx]
    c["nab"] = np.ascontiguousarray(nab.transpose(0, 2, 1, 3)).astype(f)
    kk = np.arange(128)[:, None]
    u = np.arange(1152)[None, :]
    bk = t5_bucket_np(kk - u + 512)
    tab = inp["t5_table"]
    c["t5s"] = np.ascontiguousarray(tab[bk].transpose(2, 0, 1)).astype(f)
    c["t5c"] = np.ascontiguousarray(np.broadcast_to(tab[[15, 31], :].T[None], (128, 8, 2))).astype(f)
    c["lam"] = np.ascontiguousarray(np.broadcast_to(inp["da_lambda"][0][None], (128, 4, 64))).astype(f)
    c["sg"] = np.ascontiguousarray(inp["da_subln_g"][0].reshape(128, 1)).astype(f)
    return c


COMMON_SHAPES = {
    "gT": [128, 2, 5, 8], "gB": [2, 2, D], "wqkv0": [128, KC, 3 * D], "wqkv1": [128, KC, 3 * D],
    "wo0": [128, KC, D], "wo1": [128, KC, D], "win": [2, 128, KC, 2 * FF], "wout": [2, 128, FC, D],
    "wg": [2, 128, KC, D], "wp": [2, 128, 2, D], "cw": [128, 2, 3, FC], "cb": [128, 2, FC],
    "nab": [16, 128, NA_NT, 256], "t5s": [8, 128, 1152], "t5c": [128, 8, 2], "lam": [128, 4, 64], "sg": [128, 1],
}


class Prog:
    def __init__(self, debug=False, stop_after=None, layers=(0, 1), da_heads=8, na_pairs=8):
        self.debug = debug
        self.stop_after = stop_after
        self.layers = layers
        self.da_heads = da_heads
        self.na_pairs = na_pairs
        nc = self.nc = bass.Bass("TRN2", target_bir_lowering=False)
        self.c = Ctx(nc)
        di = lambda n, s, dt=F32: nc.dram_tensor(n, list(s), dt, kind="ExternalInput").ap()
        self.x = di("x", [S, D])
        self.pT = di("pT", [2, 128, 2, S])
        self.w = {k: di(k, s) for k, s in COMMON_SHAPES.items()}
        self.y = nc.dram_tensor("y", [S, D], F32, kind="ExternalOutput").ap()
        sk = "ExternalOutput" if debug else "Internal"
        self.xs1 = nc.dram_tensor("xs1", [S, D], F32, kind=sk).ap()
        self.xs2 = nc.dram_tensor("xs2", [S, D], F32, kind=sk).ap()
        self.oT = nc.dram_tensor("oT", [8, 128, S], BF16, kind=sk).ap()
        self.aT = nc.dram_tensor("aT", [FC, 128, S], BF16, kind=sk).ap()
        c = self.c
        self.d_xs1 = c.dram("xs1")
        self.d_xs2 = c.dram("xs2")
        self.d_oT = c.dram("oT")
        self.d_aT = c.dram("aT")
        self.d_y = c.dram("y")
        self.build()

    def consts(self):
        c = self.c
        self.identf = c.sb("identf", [128, 128], F32)
        self.ident = c.sb("ident", [128, 128], BF16)
        self.ones_b = c.sb("ones_b", [128, 128], BF16)
        self.ones_f = c.sb("ones_f", [128, 128], F32)
        self.nhalf = c.sb("nhalf", [128, 512], F32)
        self.gT = c.sb("gT", [128, 2, 5, 8], F32, dma=True)
        self.cw = c.sb("cw", [128, 2, 3, FC], F32, dma=True)
        self.cb = c.sb("cb", [128, 2, FC], F32, dma=True)
        idf, idb = self.identf, self.ident
        c.op("pool", lambda e: e.memset(idf.t[:], 0.0), writes=[idf])
        c.op("pool", lambda e: e.affine_select(out=idf.t[:], in_=idf.t[:], pattern=[[-1, 128]],
                                               compare_op=ALU.not_equal, fill=1.0, base=0, channel_multiplier=1),
             reads=[idf], writes=[idf])
        c.op("dve", lambda e: e.tensor_copy(out=idb.t[:], in_=idf.t[:]), reads=[idf], writes=[idb])
        c.op("dve", lambda e: e.memset(self.ones_b.t[:], 1.0), writes=[self.ones_b])
        c.op("dve", lambda e: e.memset(self.ones_f.t[:], 1.0), writes=[self.ones_f])
        c.op("dve", lambda e: e.memset(self.nhalf.t[:], -0.5), writes=[self.nhalf])
        c.dma("sp", self.gT.t[:], self.w["gT"], self.gT, writes=[self.gT])
        c.dma("sp", self.cw.t[:], self.w["cw"], self.cw, writes=[self.cw])
        c.dma("sp", self.cb.t[:], self.w["cb"], self.cb, writes=[self.cb])

    def rstd_from_ss(self, ss_ap, out_ap, bufs, n):
        c = self.c
        c.op("dve", lambda e: e.tensor_scalar(out=out_ap, in0=ss_ap, scalar1=1.0 / n, scalar2=EPS,
                                              op0=ALU.mult, op1=ALU.add), reads=bufs, writes=bufs)
        c.op("pool", lambda e: e.tensor_tensor(out=out_ap, in0=out_ap, in1=self.nhalf.t[:, 0:1], op=ALU.pow),
             reads=list(bufs) + [self.nhalf], writes=bufs)

    def load_weight(self, dst, dst_ap_fn, src_ap_fn, nk, ncols, stage_ring, gcol_fn=None, engs=("dve",)):
        c = self.c
        scol = stage_ring.bufs[0].shape[-1]
        skc = stage_ring.bufs[0].shape[1]
        n = 0
        for k0 in range(0, nk, skc):
            k1 = min(nk, k0 + skc)
            for c0 in range(0, ncols, scol):
                c1 = min(ncols, c0 + scol)
                st = stage_ring.next()
                c.dma("sp", st.t[:, 0:k1 - k0, 0:c1 - c0], src_ap_fn(k0, k1, c0, c1), st, writes=[st])
                for k in range(k0, k1):
                    eng = engs[n % len(engs)]
                    n += 1
                    src_ap = st.t[:, k - k0, 0:c1 - c0]
                    out_ap = dst_ap_fn(k, c0, c1)
                    if gcol_fn is not None:
                        g_ap, g_buf = gcol_fn(k)
                        if eng == "act":
                            c.op("act", lambda e, o=out_ap, i=src_ap, g=g_ap: e.activation(out=o, in_=i, func=AF.Copy, scale=g),
                                 reads=[st, g_buf], writes=[dst])
                        else:
                            c.op(eng, lambda e, o=out_ap, i=src_ap, g=g_ap: e.tensor_scalar(
                                out=o, in0=i, scalar1=g, scalar2=None, op0=ALU.mult), reads=[st, g_buf], writes=[dst])
                    else:
                        if eng == "act":
                            c.op("act", lambda e, o=out_ap, i=src_ap: e.activation(out=o, in_=i, func=AF.Copy),
                                 reads=[st], writes=[dst])
                        else:
                            c.op(eng, lambda e, o=out_ap, i=src_ap: e.tensor_copy(out=o, in_=i), reads=[st], writes=[dst])

    def pipeline(self, n, stages, skews):
        order = sorted(zip(stages, skews), key=lambda fs: -fs[1])
        for i in range(n + max(skews)):
            for fn, sk in order:
                t = i - sk
                if 0 <= t < n:
                    fn(t)

    def stats(self, src_ap, src_bufs, st, col, junk):
        c = self.c
        c.op("act", lambda e: e.activation(out=junk.t[:], in_=src_ap, func=AF.Square, accum_out=st.t[:, col:col + 1]),
             reads=list(src_bufs) + [st], writes=[st])
        self.rstd_from_ss(st.t[:, col:col + 1], st.t[:, col + 1:col + 2], [st], D)

    def norm_part(self, xt, ring_junk, ring_hb, stat, si):
        c = self.c
        junk = ring_junk.bufs[0]
        c.op("act", lambda e: e.activation(out=junk.t[:], in_=xt.t[:], func=AF.Square, accum_out=stat.t[:, si:si + 1]),
             reads=[xt], writes=[stat])
        self.rstd_from_ss(stat.t[:, si:si + 1], stat.t[:, si + 1:si + 2], [stat], D)
        hb = ring_hb.next()
        c.op("act", lambda e: e.activation(out=hb.t[:], in_=xt.t[:], func=AF.Copy, scale=stat.t[:, si + 1:si + 2]),
             reads=[xt, stat], writes=[hb])
        return hb

    def transpose_part(self, hb, hT, tcol, ring_pT, eng="dve"):
        c = self.c
        pT = ring_pT.next()
        for k in range(KC):
            c.op("pe", lambda e, k=k: e.transpose(out=pT.t[:, k, :], in_=hb.t[:, k * 128:(k + 1) * 128],
                                                  identity=self.ident.t[:]),
                 reads=[hb, self.ident], writes=[pT], inc=(k == KC - 1))
        if eng == "act":
            c.op("act", lambda e: e.activation(out=hT.t[:, :, tcol:tcol + 128], in_=pT.t[:], func=AF.Copy), reads=[pT], writes=[hT])
        else:
            c.op("dve", lambda e: e.tensor_copy(out=hT.t[:, :, tcol:tcol + 128], in_=pT.t[:]), reads=[pT], writes=[hT])

    def norm_transpose(self, xt, hT, tcol, ring_junk, ring_hb, ring_pT, stat, si):
        hb = self.norm_part(xt, ring_junk, ring_hb, stat, si)
        self.transpose_part(hb, hT, tcol, ring_pT)

    def phase1(self, L, x_src, d_src, hT):
        c = self.c
        c.push()
        xr = c.ring("p1x", 4, [128, D], F32, dma=True)
        junk = c.sb("p1j", [128, D], BF16)
        hbr = c.ring("p1h", 3, [128, D], BF16)
        pT = c.ring("p1p", 2, [128, KC, 128], BF16, psum=True)
        statr = c.ring("p1s", 6, [128, 4], F32)
        xl, sts, hbs = {}, {}, {}

        def issue(t):
            xt = xr.next()
            c.dma("sp", xt.t[:], x_src[t * 128:(t + 1) * 128, :], xt, writes=[xt])
            xl[t] = xt

        def s1(t):
            if t + 2 < NT:
                issue(t + 2)
            st = statr.next()
            c.op("dve", lambda e: e.memset(st.t[:], 0.0), writes=[st])
            self.stats(xl[t].t[:], [xl[t]], st, 0, junk)
            sts[t] = st

        def s2(t):
            xt, st = xl.pop(t), sts.pop(t)
            hb = hbr.next()
            c.op("act", lambda e: e.activation(out=hb.t[:], in_=xt.t[:], func=AF.Copy, scale=st.t[:, 1:2]),
                 reads=[xt, st], writes=[hb])
            hbs[t] = hb

        def s3(t):
            self.transpose_part(hbs.pop(t), hT, t * 128, pT)

        for t in range(2):
            issue(t)
        self.pipeline(NT, [s1, s2, s3], [0, 1, 2])
        c.pop()

    def phase2_na(self, L, hT):
        c = self.c
        c.push()
        wst = c.ring("wst", 2, [128, KC, 128], F32, dma=True)
        wbr = c.ring("wb", 2, [128, KC, 384], BF16)
        QT = c.sb("QT", [128, S], BF16)
        KT = c.sb("KT", [128, 2, S], BF16)
        c.op("pool", lambda e: e.memset(KT.t[:], 0.0), writes=[KT])
        Vaug = c.sb("Vaug", [128, 2, NT, 128], BF16)
        c.op("pool", lambda e: e.memset(Vaug.t[:], 1.0), writes=[Vaug])
        nst = c.sb("nst", [128, NA_NT, 256], F32, dma=True)
        nb = [c.sb("nb%d" % a, [128, NA_NT, 256], BF16) for a in range(2)]
        PT = c.ring("PT", 3, [128, 1536], BF16)
        OTp = c.ring("OTp", 2, [128, S], BF16, dma=True)
        rr = c.ring("rr", 2, [128, 512], F32)
        pS = c.ring("pS", 2, [128, 1536], F32, psum=True)
        pO = [c.ps("pO%d" % a, [128, 512], F32) for a in range(2)]

        def v_evac(p, tb):
            pv = p.t[:, 0:512].rearrange("p (a b) -> p a b", a=4)
            c.op("dve", lambda e: e.tensor_copy(out=Vaug.t[:, 0, tb * 4:(tb + 1) * 4, 0:64], in_=pv[:, :, 0:64]),
                 reads=[p], writes=[Vaug])
            c.op("dve", lambda e: e.tensor_copy(out=Vaug.t[:, 1, tb * 4:(tb + 1) * 4, 64:128], in_=pv[:, :, 64:128]),
                 reads=[p], writes=[Vaug])

        wb_next = self.load_qkv_w(L, 0, self.w["wqkv0"], wst, wbr)
        for hp in range(self.na_pairs):
            wb = wb_next
            self.qkv_compute(wb, hT, QT, KT, None, pS, v_evac=v_evac)
            if hp + 1 < self.na_pairs:
                wb_next = self.load_qkv_w(L, hp + 1, self.w["wqkv0"], wst, wbr)
            for a in range(2):
                h = hp * 2 + a
                c.dma("sp", nst.t[:], self.w["nab"][h], nst, writes=[nst])
                c.op("pool", lambda e, a=a: e.tensor_copy(out=nb[a].t[:], in_=nst.t[:]), reads=[nst], writes=[nb[a]])
            ot = OTp.next()
            units = [(J, a, pp) for J in range(NB) for a in range(2) for pp in range(2)]
            state = {}

            def stage_a(u):
                J, a, pp = units[u]
                P = 2 * J + pp
                chunks = na_chunks(P)
                ps = pS.next()
                n = len(chunks)
                for i, ch in enumerate(chunks):
                    tid = na_tile_id(P, ch)
                    c.op("pe", lambda e, i=i, ch=ch: e.matmul(
                        ps.t[:, i * 256:(i + 1) * 256], lhsT=KT.t[:, a, ch * 128:(ch + 1) * 128],
                        rhs=QT.t[:, P * 256:(P + 1) * 256], start=True, stop=False),
                        reads=[KT, QT], writes=[ps], inc=False)
                    c.op("pe", lambda e, i=i, tid=tid: e.matmul(
                        ps.t[:, i * 256:(i + 1) * 256], lhsT=self.ident.t[:], rhs=nb[a].t[:, tid, :],
                        start=False, stop=True), reads=[self.ident, nb[a]], writes=[ps], inc=(i == n - 1))
                pt = PT.next()
                c.op("act", lambda e: e.activation(out=pt.t[:, 0:n * 256], in_=ps.t[:, 0:n * 256], func=AF.Exp),
                     reads=[ps], writes=[pt])
                state[u] = (pt, chunks)

            def stage_b(u):
                J, a, pp = units[u]
                pt, chunks = state.pop(u)
                n = len(chunks)
                for i, ch in enumerate(chunks):
                    c.op("pe", lambda e, i=i, ch=ch: e.matmul(
                        pO[a].t[:, pp * 256:(pp + 1) * 256], lhsT=Vaug.t[:, a, ch, :], rhs=pt.t[:, i * 256:(i + 1) * 256],
                        start=(i == 0), stop=(i == n - 1)), reads=[Vaug, pt], writes=[pO[a]], inc=(i == n - 1))
                if pp == 1:
                    ol, oh = 64 * a, 64 * a + 64
                    sl, sh = 64 * (1 - a), 64 * (1 - a) + 64
                    r = rr.next()
                    c.op("dve", lambda e: e.reciprocal(out=r.t[ol:oh, :], in_=pO[a].t[sl:sh, :]),
                         reads=[pO[a]], writes=[r])
                    c.op("dve", lambda e: e.tensor_tensor(out=ot.t[ol:oh, J * 512:(J + 1) * 512], in0=pO[a].t[ol:oh, :],
                                                          in1=r.t[ol:oh, :], op=ALU.mult),
                         reads=[pO[a], r], writes=[ot])

            nu = len(units)
            stage_a(0)
            for u in range(nu):
                if u + 1 < nu:
                    stage_a(u + 1)
                stage_b(u)
            c.dma("sp", self.oT[hp], ot.t[:], ot, reads=[ot])
        c.pop()

    def load_qkv_w(self, L, hp, wq_dram, wst, wbr):
        wb = wbr.next()
        for part in range(3):
            col0 = part * D + hp * 128
            self.load_weight(wb, lambda k, c0, c1, part=part: wb.t[:, k, part * 128 + c0:part * 128 + c1],
                             lambda k0, k1, c0, c1, col0=col0: wq_dram[:, k0:k1, col0 + c0:col0 + c1],
                             KC, 128, wst, gcol_fn=lambda k: (self.gT.t[:, L, 0, k:k + 1], self.gT))
        return wb

    def qkv_compute(self, wb, hT, QT, KT, V, pS, view=None, v_evac=None):
        c = self.c
        if view is None:
            view = lambda p, a=0, b=512: p.t[a:b, 0:512] if False else p.t[:, 0:512]
        for tb in range(NB):
            for part, dst in ((0, QT), (1, KT)):
                p = pS.next()
                for k in range(KC):
                    c.op("pe", lambda e, k=k, part=part, p=p: e.matmul(
                        view(p), lhsT=wb.t[:, k, part * 128:(part + 1) * 128], rhs=hT.t[:, k, tb * 512:(tb + 1) * 512],
                        start=(k == 0), stop=(k == KC - 1)), reads=[wb, hT], writes=[p], inc=(k == KC - 1))
                if part == 0:
                    c.op("act", lambda e, p=p: e.activation(out=QT.t[:, tb * 512:(tb + 1) * 512], in_=view(p),
                                                            func=AF.Copy, scale=0.125), reads=[p], writes=[QT])
                else:
                    c.op("dve", lambda e, p=p: e.tensor_copy(out=KT.t[0:64, 0, tb * 512:(tb + 1) * 512], in_=view(p)[0:64, :]),
                         reads=[p], writes=[KT])
                    c.op("dve", lambda e, p=p: e.tensor_copy(out=KT.t[64:128, 1, tb * 512:(tb + 1) * 512], in_=view(p)[64:128, :]),
                         reads=[p], writes=[KT])
            p = pS.next()
            for tt in range(4):
                t = tb * 4 + tt
                for k in range(KC):
                    c.op("pe", lambda e, k=k, t=t, tt=tt, p=p: e.matmul(
                        view(p)[:, tt * 128:(tt + 1) * 128], lhsT=hT.t[:, k, t * 128:(t + 1) * 128], rhs=wb.t[:, k, 256:384],
                        start=(k == 0), stop=(k == KC - 1)), reads=[wb, hT], writes=[p], inc=(k == KC - 1 and tt == 3))
            if v_evac is not None:
                v_evac(p, tb)
            else:
                c.op("dve", lambda e, p=p: e.tensor_copy(out=V.t[:, tb * 4:(tb + 1) * 4, :],
                                                         in_=view(p).rearrange("p (a b) -> p a b", a=4)),
                     reads=[p], writes=[V])

    def phase2_da(self, L, hT):
        c = self.c
        c.push()
        wst = c.ring("wst", 2, [128, KC, 128], F32, dma=True)
        wbr = c.ring("wb", 2, [128, KC, 384], BF16)
        QT = c.sb("QT", [128, S], BF16)
        KT = c.sb("KT", [128, 2, S], BF16)
        c.op("pool", lambda e: e.memset(KT.t[:], 0.0), writes=[KT])
        V3 = c.sb("V3", [128, 3, NT, 128], BF16)
        eb = c.sb("eb", [128, 8, 2], F32)
        onesx = c.sb("onesx", [128, 2, 128], BF16)
        sst = c.sb("sst", [128, 1152], F32, dma=True)
        strip = c.sb("strip", [128, 1152], BF16)
        t5c = c.sb("t5c", [128, 8, 2], F32, dma=True)
        lam = c.sb("lam", [128, 4, 64], F32, dma=True)
        sg = c.sb("sg", [128, 1], F32, dma=True)
        lj = c.sb("lj", [128, 2, 64], F32)
        lv = c.sb("lv", [128, 8], F32)
        PT = c.ring("PT", DA_PT, [128, 512], BF16)
        OTh = c.ring("OTh", 2, [128, S], BF16, dma=True)
        tmp = c.ring("tmp", 10, [128, 512], F32)
        accr = c.ring("acc", 4, [128, 512], F32)
        SUM_PE_EVERY = DA_SUM_EVERY
        pS = c.ring("pS", DA_DEPTH + 1, [128, 512], F32, psum=True)
        pO = [c.ps("pO%d" % a, [128, 512], F32) for a in range(2)]
        pSum = [c.ps("pSum%d" % a, [128, 512], F32) for a in range(2)]
        c.dma("sp", t5c.t[:], self.w["t5c"], t5c, writes=[t5c])
        c.dma("sp", lam.t[:], self.w["lam"], lam, writes=[lam])
        c.dma("sp", sg.t[:], self.w["sg"], sg, writes=[sg])
        c.op("act", lambda e: e.activation(out=eb.t[:], in_=t5c.t[:], func=AF.Exp), reads=[t5c], writes=[eb])
        c.op("dve", lambda e: e.memset(lv.t[:], 0.0), writes=[lv])
        c.op("dve", lambda e: e.tensor_tensor(out=lj.t[:, 0, :], in0=lam.t[:, 0, :], in1=lam.t[:, 1, :], op=ALU.mult),
             reads=[lam], writes=[lj])
        c.op("dve", lambda e: e.tensor_tensor(out=lj.t[:, 1, :], in0=lam.t[:, 2, :], in1=lam.t[:, 3, :], op=ALU.mult),
             reads=[lam, lj], writes=[lj])
        c.op("dve", lambda e: e.reduce_sum(out=lv.t[:, 0:2], in_=lj.t[:], axis=mybir.AxisListType.X), reads=[lj], writes=[lv])
        c.op("act", lambda e: e.activation(out=lv.t[:, 2:4], in_=lv.t[:, 0:2], func=AF.Exp), reads=[lv], writes=[lv])
        c.op("dve", lambda e: e.tensor_tensor(out=lv.t[:, 4:5], in0=lv.t[:, 3:4], in1=lv.t[:, 2:3], op=ALU.subtract),
             reads=[lv], writes=[lv])
        c.op("dve", lambda e: e.tensor_scalar(out=lv.t[:, 4:5], in0=lv.t[:, 4:5], scalar1=-LAMBDA_INIT1, scalar2=None,
                                              op0=ALU.add), reads=[lv], writes=[lv])
        c.op("dve", lambda e: e.tensor_scalar(out=lv.t[:, 5:6], in0=sg.t[:], scalar1=1.0 - LAMBDA_INIT1, scalar2=None,
                                              op0=ALU.mult), reads=[lv, sg], writes=[lv])
        wb_next = self.load_qkv_w(L, 0, self.w["wqkv1"], wst, wbr)
        deferred = []

        def run_deferred(force=False):
            items = deferred[:]
            del deferred[:]
            for item in items:
                item[0] -= 1
                if item[0] <= 0 or force:
                    item[1]()
                else:
                    deferred.append(item)

        for h in range(self.da_heads):
            wb = wb_next

            def v_evac(p, tb, h=h):
                pv = p.t[:, 0:512].rearrange("p (a b) -> p a b", a=4)
                c.op("dve", lambda e: e.tensor_copy(out=V3.t[:, 0, tb * 4:(tb + 1) * 4, :], in_=pv), reads=[p], writes=[V3])
                for side in range(2):
                    c.op("act", lambda e, side=side: e.activation(out=V3.t[:, 1 + side, tb * 4:(tb + 1) * 4, :], in_=pv,
                                                                  func=AF.Copy, scale=eb.t[:, h, side:side + 1]),
                         reads=[p, eb], writes=[V3])
            self.qkv_compute(wb, hT, QT, KT, None, pS, v_evac=v_evac)
            for side in range(2):
                c.op("dve", lambda e, side=side: e.tensor_scalar(out=onesx.t[:, side, :], in0=self.ones_b.t[:],
                                                                 scalar1=eb.t[:, h, side:side + 1], scalar2=None, op0=ALU.mult),
                     reads=[self.ones_b, eb], writes=[onesx])
            if h + 1 < 8:
                wb_next = self.load_qkv_w(L, h + 1, self.w["wqkv1"], wst, wbr)
            c.dma("sp", sst.t[:], self.w["t5s"][h], sst, writes=[sst])
            c.op("pool", lambda e: e.tensor_copy(out=strip.t[:], in_=sst.t[:]), reads=[sst], writes=[strip])
            ot = OTh.next()
            steps = [(J, m, kc) for J in range(NB) for m in range(2) for kc in range(NT)]
            state = {}
            tts = {}
            accs = {}

            def stage_a(s):
                J, m, kc = steps[s]
                pl, ph = 64 * m, 64 * m + 64
                off = kc * 128 - J * 512
                near = -256 < off < 640
                ps = pS.next()
                c.op("pe", lambda e: e.matmul(
                    ps.t[:], lhsT=KT.t[:, m, kc * 128:(kc + 1) * 128], rhs=QT.t[:, J * 512:(J + 1) * 512],
                    start=True, stop=not near), reads=[KT, QT], writes=[ps], inc=not near)
                if near:
                    u0 = 512 - off
                    c.op("pe", lambda e: e.matmul(
                        ps.t[:], lhsT=self.ident.t[:], rhs=strip.t[:, u0:u0 + 512], start=False, stop=True),
                        reads=[self.ident, strip], writes=[ps])
                pt = PT.next()
                c.op("act", lambda e: e.activation(out=pt.t[:], in_=ps.t[:], func=AF.Exp), reads=[ps], writes=[pt])
                state[s] = (pt, 0 if near else (1 if off < 0 else 2))

            def post_m(J, m):
                r = tmp.next()
                c.op("dve", lambda e: e.reciprocal(out=r.t[:], in_=pSum[m].t[:]), reads=[pSum[m]], writes=[r])
                t_ = tmp.next()
                c.op("dve", lambda e: e.tensor_tensor(out=t_.t[:], in0=pO[m].t[:], in1=r.t[:], op=ALU.mult),
                     reads=[pO[m], r], writes=[t_])
                tts[(J, m)] = t_

            def post_j(J, ot):
                t0_, t1_ = tts.pop((J, 0)), tts.pop((J, 1))
                o = tmp.next()
                c.op("dve", lambda e: e.scalar_tensor_tensor(out=o.t[:], in0=t1_.t[:], scalar=lv.t[:, 4:5], in1=t0_.t[:],
                                                             op0=ALU.mult, op1=ALU.add), reads=[t0_, t1_, lv], writes=[o])
                sq = tmp.next()
                c.op("pool", lambda e: e.tensor_tensor(out=sq.t[:], in0=o.t[:], in1=o.t[:], op=ALU.mult),
                     reads=[o], writes=[sq])

                def fin():
                    pst = pS.next()
                    c.op("pe", lambda e: e.matmul(pst.t[:], lhsT=self.ones_f.t[:], rhs=sq.t[:], start=True, stop=True),
                         reads=[self.ones_f, sq], writes=[pst])
                    c.op("dve", lambda e: e.tensor_scalar(out=sq.t[:], in0=pst.t[:], scalar1=1.0 / 128, scalar2=EPS,
                                                          op0=ALU.mult, op1=ALU.add), reads=[pst], writes=[sq])
                    c.op("act", lambda e: e.activation(out=sq.t[:], in_=sq.t[:], func=AF.Ln), reads=[sq], writes=[sq])
                    c.op("act", lambda e: e.activation(out=sq.t[:], in_=sq.t[:], func=AF.Exp, scale=-0.5), reads=[sq], writes=[sq])
                    c.op("dve", lambda e: e.scalar_tensor_tensor(
                        out=ot.t[:, J * 512:(J + 1) * 512], in0=o.t[:], scalar=lv.t[:, 5:6], in1=sq.t[:],
                        op0=ALU.mult, op1=ALU.mult), reads=[o, sq, lv], writes=[ot])
                deferred.append([6, fin])

            def stage_b(s, ot, h=h):
                J, m, kc = steps[s]
                pt, side = state.pop(s)
                c.op("pe", lambda e: e.matmul(
                    pO[m].t[:], lhsT=V3.t[:, side, kc, :], rhs=pt.t[:], start=(kc == 0), stop=(kc == NT - 1)),
                    reads=[V3, pt], writes=[pO[m]], inc=(kc % SUM_PE_EVERY != 0))
                if kc % SUM_PE_EVERY == 0:
                    lw = self.ones_b.t[:] if side == 0 else onesx.t[:, side - 1, :]
                    c.op("pe", lambda e: e.matmul(
                        pSum[m].t[:], lhsT=lw, rhs=pt.t[:], start=(kc == 0), stop=False),
                        reads=[self.ones_b, onesx, pt], writes=[pSum[m]])
                else:
                    first = (J, m) not in accs
                    if DA_SKIP_DVE and not first:
                        return_early = True
                    else:
                        return_early = False
                    if first:
                        accs[(J, m)] = accr.next()
                    ac = accs[(J, m)]
                    if first and side == 0:
                        c.op("dve", lambda e: e.tensor_copy(out=ac.t[:], in_=pt.t[:]), reads=[pt], writes=[ac])
                    elif first:
                        c.op("dve", lambda e: e.tensor_scalar(out=ac.t[:], in0=pt.t[:], scalar1=eb.t[:, h, side - 1:side],
                                                              scalar2=None, op0=ALU.mult), reads=[pt, eb], writes=[ac])
                    elif return_early:
                        pass
                    elif side == 0:
                        c.op("dve", lambda e: e.tensor_tensor(out=ac.t[:], in0=ac.t[:], in1=pt.t[:], op=ALU.add),
                             reads=[pt, ac], writes=[ac])
                    else:
                        c.op("dve", lambda e: e.scalar_tensor_tensor(out=ac.t[:], in0=pt.t[:], scalar=eb.t[:, h, side - 1:side],
                                                                     in1=ac.t[:], op0=ALU.mult, op1=ALU.add),
                             reads=[pt, ac, eb], writes=[ac])
                if kc == NT - 1:
                    def fin_m(J=J, m=m):
                        ac = accs.pop((J, m))
                        c.op("pe", lambda e: e.matmul(pSum[m].t[:], lhsT=self.ones_f.t[:], rhs=ac.t[:], start=False, stop=True),
                             reads=[self.ones_f, ac], writes=[pSum[m]])
                        post_m(J, m)
                        if m == 1:
                            post_j(J, ot)
                    deferred.append([3, fin_m])

            ns = len(steps)
            DEPTH = DA_DEPTH
            for s in range(min(DEPTH, ns)):
                stage_a(s)
            for s in range(ns):
                if s + DEPTH < ns:
                    stage_a(s + DEPTH)
                stage_b(s, ot)
                run_deferred()
            while deferred:
                run_deferred(force=True)
            c.dma("sp", self.oT[h], ot.t[:], ot, reads=[ot])
        c.pop()

    def phase3(self, L, x_src, d_src, hT):
        c = self.c
        c.push()
        wo = c.sb("wo", [128, KC, D], BF16)
        g1 = c.sb("g1", [128, D], F32, dma=True)
        junk = c.sb("junk", [128, D], BF16)
        wsrc = self.w["wo0"] if L == 0 else self.w["wo1"]
        c.dma("sp", g1.t[:], self.w["gB"][L, 0, :].partition_broadcast(128), g1, writes=[g1])
        c.push()
        wst = c.ring("wst", 2, [128, KC, 512], F32, dma=True)
        self.load_weight(wo, lambda k, c0, c1: wo.t[:, k, c0:c1], lambda k0, k1, c0, c1: wsrc[:, k0:k1, c0:c1],
                         KC, D, wst, engs=("dve", "act"))
        c.pop()
        OTb = c.ring("OTb", 2, [128, 8, 512], BF16, dma=True)
        xr = c.ring("xr", 5, [128, D], F32, dma=True)
        tmp = c.ring("tmp", 2, [128, D], F32)
        x1r = c.ring("x1", 4, [128, D], F32, dma=True)
        hbr = c.ring("hb", 3, [128, D], BF16)
        statr = c.ring("stat", 8, [128, 4], F32)
        pm = c.ring("pm", 3, [128, 2, 512], F32, psum=True)
        pT = c.ring("pT", 2, [128, KC, 128], BF16, psum=True)
        xl, obl, ps_, sts, x1s, hbs = {}, {}, {}, {}, {}, {}

        def issue(t):
            xt = xr.next()
            c.dma("sp", xt.t[:], x_src[t * 128:(t + 1) * 128, :], xt, writes=[xt])
            xl[t] = xt

        def issue_b(tb):
            ob = OTb.next()
            c.dma("sp", ob.t[:], self.oT[:, :, tb * 512:(tb + 1) * 512].rearrange("h p t -> p h t"), ob,
                  writes=[ob])
            obl[tb] = ob

        def s1a(t):
            tb, tt = t // 4, t % 4
            if tt == 0 and tb + 1 < NB:
                issue_b(tb + 1)
            if t + 3 < NT:
                issue(t + 3)
            ob = obl[tb]
            p = pm.next()
            for half in range(2):
                for k in range(KC):
                    c.op("pe", lambda e, k=k, half=half: e.matmul(
                        p.t[:, half, :], lhsT=ob.t[:, k, tt * 128:(tt + 1) * 128], rhs=wo.t[:, k, half * 512:(half + 1) * 512],
                        start=(k == 0), stop=(k == KC - 1)), reads=[ob, wo], writes=[p], inc=(k == KC - 1 and half == 1))
            ps_[t] = p

        def s1b(t):
            p = ps_[t]
            st = statr.next()
            c.op("dve", lambda e: e.memset(st.t[:], 0.0), writes=[st])
            self.stats(p.t[:].rearrange("p a b -> p (a b)"), [p], st, 0, junk)
            sts[t] = st

        def s2a(t):
            p, st, xt = ps_.pop(t), sts[t], xl.pop(t)
            tm = tmp.next()
            c.op("dve", lambda e: e.scalar_tensor_tensor(out=tm.t[:], in0=p.t[:].rearrange("p a b -> p (a b)"),
                                                         scalar=st.t[:, 1:2], in1=g1.t[:],
                                                         op0=ALU.mult, op1=ALU.mult), reads=[p, st, g1], writes=[tm])
            x1 = x1r.next()
            c.op("pool", lambda e: e.tensor_tensor(out=x1.t[:], in0=tm.t[:], in1=xt.t[:], op=ALU.add),
                 reads=[tm, xt], writes=[x1])
            c.dma("sp", self.xs1[t * 128:(t + 1) * 128, :], x1.t[:], x1, reads=[x1])
            x1s[t] = x1

        def s2b(t):
            self.stats(x1s[t].t[:], [x1s[t]], sts[t], 2, junk)

        def s3(t):
            x1, st = x1s.pop(t), sts.pop(t)
            hb = hbr.next()
            c.op("act", lambda e: e.activation(out=hb.t[:], in_=x1.t[:], func=AF.Copy, scale=st.t[:, 3:4]),
                 reads=[x1, st], writes=[hb])
            hbs[t] = hb

        def s4(t):
            self.transpose_part(hbs.pop(t), hT, t * 128, pT)

        issue_b(0)
        for t in range(3):
            issue(t)
        self.pipeline(NT, [s1a, s1b, s2a, s2b, s3, s4], [0, 1, 2, 3, 4, 5])
        c.pop()

    def post_norm_residual(self, p, xt, gB, xo, tmp_ring, junk_ring, stat, si):
        c = self.c
        junk = junk_ring.bufs[0]
        c.op("act", lambda e: e.activation(out=junk.t[:], in_=p.t[:].rearrange("p a b -> p (a b)"), func=AF.Square,
                                           accum_out=stat.t[:, si:si + 1]), reads=[p], writes=[stat])
        self.rstd_from_ss(stat.t[:, si:si + 1], stat.t[:, si + 1:si + 2], [stat], D)
        tm = tmp_ring.next()
        c.op("dve", lambda e: e.scalar_tensor_tensor(out=tm.t[:], in0=p.t[:].rearrange("p a b -> p (a b)"),
                                                     scalar=stat.t[:, si + 1:si + 2], in1=gB.t[:],
                                                     op0=ALU.mult, op1=ALU.mult), reads=[p, stat, gB], writes=[tm])
        c.op("pool", lambda e: e.tensor_tensor(out=xo.t[:], in0=tm.t[:], in1=xt.t[:], op=ALU.add),
             reads=[tm, xt], writes=[xo])

    def phase4a(self, L, hT):
        c = self.c
        c.push()
        wst = c.ring("wst", 2, [128, KC, 384], F32, dma=True)
        wir = c.ring("wi", 2, [128, KC, 2 * 768], BF16)
        gsb = c.ring("gsb", 2, [128, 514], F32)
        t1 = c.ring("t1", 2, [128, 512], F32)
        t2 = c.ring("t2", 2, [128, 512], F32)
        ge = c.ring("ge", 2, [128, 512], F32)
        aTb = c.ring("aTb", 2, [128, 6, 512], BF16, dma=True)
        pg = c.ring("pg", 2, [128, 512], F32, psum=True)
        pv = c.ring("pv", 2, [128, 512], F32, psum=True)
        win = self.w["win"][L]
        def load_group(gi):
            fc0, nf = FGROUPS[gi]
            wi = wir.next()
            for part in range(2):
                col0 = part * FF + fc0 * 128
                self.load_weight(wi, lambda k, c0, c1, part=part: wi.t[:, k, part * 768 + c0:part * 768 + c1],
                                 lambda k0, k1, c0, c1, col0=col0: win[:, k0:k1, col0 + c0:col0 + c1],
                                 KC, nf * 128, wst, gcol_fn=lambda k: (self.gT.t[:, L, 2, k:k + 1], self.gT), engs=("act",))
            return wi
        BW = 510
        blocks = [(s0, min(BW, S - s0)) for s0 in range(0, S, BW)]
        def load_group_thunks(gi):
            fc0_, nf_ = FGROUPS[gi]
            wi_ = wir.next()
            thunks = []
            for part in range(2):
                col0 = part * FF + fc0_ * 128
                for c0 in range(0, nf_ * 128, 384):
                    c1 = min(nf_ * 128, c0 + 384)
                    hold = {}

                    def dma_t(col0=col0, c0=c0, c1=c1, hold=hold):
                        st = wst.next()
                        hold["st"] = st
                        c.dma("sp", st.t[:, 0:KC, 0:c1 - c0], win[:, 0:KC, col0 + c0:col0 + c1], st, writes=[st])
                    thunks.append(dma_t)
                    for k in range(KC):
                        def conv_t(k=k, part=part, c0=c0, c1=c1, hold=hold):
                            st = hold["st"]
                            c.op("act", lambda e: e.activation(out=wi_.t[:, k, part * 768 + c0:part * 768 + c1],
                                                               in_=st.t[:, k, 0:c1 - c0], func=AF.Copy,
                                                               scale=self.gT.t[:, L, 2, k:k + 1]),
                                 reads=[st, self.gT], writes=[wi_])
                        thunks.append(conv_t)
            return wi_, thunks

        wi_next = load_group(0)
        for gi, (fc0, nf) in enumerate(FGROUPS):
            wi = wi_next
            pending = []
            if gi + 1 < len(FGROUPS):
                wi_next, pending = load_group_thunks(gi + 1)
            for (s0, W) in blocks:
                glo, ghi = max(s0 - 1, 0), min(s0 + W + 1, S)
                gn = ghi - glo
                goff = glo - (s0 - 1)
                ab = aTb.next()
                for fi in range(nf):
                    fc = fc0 + fi
                    if pending:
                        pending.pop(0)()
                    g_ps = pg.next()
                    v_ps = pv.next()
                    for k in range(KC):
                        c.op("pe", lambda e, k=k: e.matmul(
                            g_ps.t[:, 0:gn], lhsT=wi.t[:, k, fi * 128:(fi + 1) * 128], rhs=hT.t[:, k, glo:ghi],
                            start=(k == 0), stop=(k == KC - 1)), reads=[wi, hT], writes=[g_ps], inc=(k == KC - 1))
                    for k in range(KC):
                        c.op("pe", lambda e, k=k: e.matmul(
                            v_ps.t[:, 0:W], lhsT=wi.t[:, k, 768 + fi * 128:768 + (fi + 1) * 128], rhs=hT.t[:, k, s0:s0 + W],
                            start=(k == 0), stop=(k == KC - 1)), reads=[wi, hT], writes=[v_ps], inc=(k == KC - 1))
                    gs = gsb.next()
                    c.op("act", lambda e: e.activation(out=gs.t[:, goff:goff + gn], in_=g_ps.t[:, 0:gn], func=AF.Copy),
                         reads=[g_ps], writes=[gs])
                    if goff == 1:
                        c.op("dve", lambda e: e.memset(gs.t[:, 0:1], 0.0), writes=[gs])
                    if goff + gn < W + 2:
                        c.op("dve", lambda e: e.memset(gs.t[:, W + 1:W + 2], 0.0), writes=[gs])
                    a1 = t1.next()
                    a2 = t2.next()
                    cwv = lambda j_: self.cw.t[:, L, j_, fc:fc + 1]
                    c.op("dve", lambda e: e.tensor_scalar(out=a1.t[:, 0:W], in0=gs.t[:, 0:W], scalar1=cwv(0),
                                                          scalar2=None, op0=ALU.mult),
                         reads=[gs, self.cw], writes=[a1])
                    c.op("dve", lambda e: e.scalar_tensor_tensor(
                        out=a2.t[:, 0:W], in0=gs.t[:, 1:W + 1], scalar=cwv(1), in1=a1.t[:, 0:W], op0=ALU.mult, op1=ALU.add),
                        reads=[gs, a1, self.cw], writes=[a2])
                    c.op("dve", lambda e: e.scalar_tensor_tensor(
                        out=a1.t[:, 0:W], in0=gs.t[:, 2:W + 2], scalar=cwv(2), in1=a2.t[:, 0:W], op0=ALU.mult, op1=ALU.add),
                        reads=[gs, a2, self.cw], writes=[a1])
                    gg = ge.next()
                    c.op("act", lambda e: e.activation(out=gg.t[:, 0:W], in_=a1.t[:, 0:W], func=AF.Gelu_apprx_tanh,
                                                       bias=self.cb.t[:, L, fc:fc + 1]),
                         reads=[a1, self.cb], writes=[gg])
                    c.op("dve", lambda e: e.tensor_tensor(out=ab.t[:, fi, 0:W], in0=v_ps.t[:, 0:W], in1=gg.t[:, 0:W], op=ALU.mult),
                         reads=[gg, v_ps], writes=[ab])
                c.dma("sp", self.aT[fc0:fc0 + nf, :, s0:s0 + W].rearrange("f p t -> p f t"), ab.t[:, 0:nf, 0:W], ab,
                      reads=[ab])
            while pending:
                pending.pop(0)()
        c.pop()

    def phase4b(self, L, y_dst, d_dst):
        c = self.c
        c.push()
        wout = c.sb("wout", [128, FC, D], BF16)
        wg = c.sb("wg", [128, KC, D], BF16)
        wp = c.sb("wp", [128, 2, D], BF16)
        g3 = c.sb("g3", [128, D], F32, dma=True)
        junk = c.sb("junk", [128, D], BF16)
        c.dma("sp", g3.t[:], self.w["gB"][L, 1, :].partition_broadcast(128), g3, writes=[g3])
        wo_src, wg_src, wp_src = self.w["wout"][L], self.w["wg"][L], self.w["wp"][L]
        c.push()
        wst = c.ring("wst", 2, [128, 2, 1024], F32, dma=True)
        self.load_weight(wout, lambda k, c0, c1: wout.t[:, k, c0:c1], lambda k0, k1, c0, c1: wo_src[:, k0:k1, c0:c1],
                         FC, D, wst, engs=("dve", "act"))
        self.load_weight(wg, lambda k, c0, c1: wg.t[:, k, c0:c1], lambda k0, k1, c0, c1: wg_src[:, k0:k1, c0:c1],
                         KC, D, wst, gcol_fn=lambda k: (self.gT.t[:, L, 4, k:k + 1], self.gT), engs=("dve", "act"))
        self.load_weight(wp, lambda k, c0, c1: wp.t[:, k, c0:c1], lambda k0, k1, c0, c1: wp_src[:, k0:k1, c0:c1],
                         2, D, wst, engs=("dve", "act"))
        c.pop()
        aTb = c.ring("aTb", 2, [128, FC, 512], BF16, dma=True)
        pst = c.ring("pst", 2, [128, 2, 512], F32, dma=True)
        pb = c.ring("pb", 3, [128, 2, 512], BF16)
        xr = c.ring("xr", 4, [128, D], F32, dma=True)
        tmp = c.ring("tmp", 2, [128, D], F32)
        x2r = c.ring("x2", 5, [128, D], F32)
        x3r = c.ring("x3", 2, [128, D], F32, dma=True)
        sgr = c.ring("sig", 2, [128, D], F32)
        hbr = c.ring("hb", 3, [128, D], BF16)
        h3T = c.ring("h3T", 2, [128, KC, 128], BF16)
        statr = c.ring("stat", 8, [128, 4], F32)
        pf = c.ring("pf", 2, [128, 2, 512], F32, psum=True)
        pT = c.ring("pT", 1, [128, KC, 128], BF16, psum=True)
        pgp = c.ring("pgp", 3, [128, 512], F32, psum=True)
        xl, bl, ps_, sts, x2s, hbs, h3s = {}, {}, {}, {}, {}, {}, {}

        def issue(t):
            xt = xr.next()
            c.dma("sp", xt.t[:], self.xs1[t * 128:(t + 1) * 128, :], xt, writes=[xt])
            xl[t] = xt

        def issue_b(tb):
            t0 = tb * 512
            ab = aTb.next()
            for f0 in range(0, FC, 8):
                f1 = min(FC, f0 + 8)
                c.dma("sp", ab.t[:, f0:f1, :], self.aT[f0:f1, :, t0:t0 + 512].rearrange("f p t -> p f t"), ab,
                      writes=[ab])
            pq = pst.next()
            c.dma("sp", pq.t[:], self.pT[L, :, :, t0:t0 + 512], pq, writes=[pq])
            pbb = pb.next()
            c.op("pool", lambda e: e.tensor_copy(out=pbb.t[:], in_=pq.t[:]), reads=[pq], writes=[pbb])
            bl[tb] = (ab, pbb)

        def s1(t):
            tb, tt = t // 4, t % 4
            if tt == 0 and tb + 1 < NB:
                issue_b(tb + 1)
            if t + 2 < NT:
                issue(t + 2)
            ab, pbb = bl[tb]
            p = pf.next()
            for half in range(2):
                for k in range(FC):
                    c.op("pe", lambda e, k=k, half=half: e.matmul(
                        p.t[:, half, :], lhsT=ab.t[:, k, tt * 128:(tt + 1) * 128], rhs=wout.t[:, k, half * 512:(half + 1) * 512],
                        start=(k == 0), stop=(k == FC - 1)), reads=[ab, wout], writes=[p], inc=(k == FC - 1 and half == 1))
            st = statr.next()
            c.op("dve", lambda e: e.memset(st.t[:], 0.0), writes=[st])
            self.stats(p.t[:].rearrange("p a b -> p (a b)"), [p], st, 0, junk)
            ps_[t], sts[t] = p, st

        def s2(t):
            p, st, xt = ps_.pop(t), sts[t], xl.pop(t)
            tm = tmp.next()
            c.op("dve", lambda e: e.scalar_tensor_tensor(out=tm.t[:], in0=p.t[:].rearrange("p a b -> p (a b)"),
                                                         scalar=st.t[:, 1:2], in1=g3.t[:],
                                                         op0=ALU.mult, op1=ALU.mult), reads=[p, st, g3], writes=[tm])
            x2 = x2r.next()
            c.op("pool", lambda e: e.tensor_tensor(out=x2.t[:], in0=tm.t[:], in1=xt.t[:], op=ALU.add),
                 reads=[tm, xt], writes=[x2])
            self.stats(x2.t[:], [x2], st, 2, junk)
            x2s[t] = x2

        def s3(t):
            x2, st = x2s[t], sts.pop(t)
            hb = hbr.next()
            c.op("act", lambda e: e.activation(out=hb.t[:], in_=x2.t[:], func=AF.Copy, scale=st.t[:, 3:4]),
                 reads=[x2, st], writes=[hb])
            hbs[t] = hb

        def s4(t):
            h3 = h3T.next()
            self.transpose_part(hbs.pop(t), h3, 0, pT, eng="act")
            h3s[t] = h3

        def s5(t):
            tb, tt = t // 4, t % 4
            x2, h3 = x2s.pop(t), h3s.pop(t)
            pbb = bl[tb][1]
            x3 = x3r.next()
            for half in range(2):
                pgate = pgp.next()
                for k in range(KC):
                    c.op("pe", lambda e, k=k: e.matmul(
                        pgate.t[:], lhsT=h3.t[:, k, :], rhs=wg.t[:, k, half * 512:(half + 1) * 512],
                        start=(k == 0), stop=(k == KC - 1)), reads=[h3, wg], writes=[pgate], inc=(k == KC - 1))
                sg_ = sgr.next()
                c.op("act", lambda e: e.activation(out=sg_.t[:, 0:512], in_=pgate.t[:], func=AF.Sigmoid),
                     reads=[pgate], writes=[sg_])
                pproj = pgp.next()
                for k in range(2):
                    c.op("pe", lambda e, k=k: e.matmul(
                        pproj.t[:], lhsT=pbb.t[:, k, tt * 128:(tt + 1) * 128], rhs=wp.t[:, k, half * 512:(half + 1) * 512],
                        start=(k == 0), stop=(k == 1)), reads=[pbb, wp], writes=[pproj], inc=(k == 1))
                c.op("dve", lambda e: e.tensor_tensor(out=sg_.t[:, 512:1024], in0=pproj.t[:], in1=sg_.t[:, 0:512], op=ALU.mult),
                     reads=[pproj, sg_], writes=[sg_])
                c.op("pool", lambda e: e.tensor_tensor(
                    out=x3.t[:, half * 512:(half + 1) * 512], in0=sg_.t[:, 512:1024], in1=x2.t[:, half * 512:(half + 1) * 512],
                    op=ALU.add), reads=[sg_, x2], writes=[x3])
            c.dma("sp", y_dst[t * 128:(t + 1) * 128, :], x3.t[:], x3, reads=[x3])

        issue_b(0)
        for t in range(2):
            issue(t)
        self.pipeline(NT, [s1, s2, s3, s4, s5], [0, 1, 2, 3, 4])
        c.pop()

    def build(self):
        c = self.c
        c.push()
        self.consts()
        srcs = [(self.x, None), (self.xs2, self.d_xs2)]
        dsts = [(self.xs2, self.d_xs2), (self.y, self.d_y)]
        for L in self.layers:
            x_src, d_src = srcs[L]
            if len(self.layers) == 1:
                x_src = self.x
            c.push()
            hT = c.sb("hT", [128, KC, S], BF16)
            self.phase1(L, x_src, d_src, hT)
            if self.stop_after == (L, 1):
                self.dump_hT(hT)
                c.pop(); break
            if L == 0:
                self.phase2_na(L, hT)
            else:
                self.phase2_da(L, hT)
            if self.stop_after == (L, 2):
                c.pop(); break
            self.phase3(L, x_src, d_src, hT)
            if self.stop_after == (L, 3):
                c.pop(); break
            self.phase4a(L, hT)
            c.pop()
            if self.stop_after == (L, 4):
                break
            y_dst, d_dst = dsts[L]
            self.phase4b(L, y_dst, d_dst)
            if self.stop_after == (L, 5):
                break
        c.pop()

    def dump_hT(self, hT):
        pass


def build_program(debug=False, stop_after=None, **kw):
    return Prog(debug=debug, stop_after=stop_after, **kw)


def make_in_maps(inputs):
    inp = {k: np.asarray(v, dtype=np.float32) for k, v in inputs.items()}
    common = prep_common(inp)
    in_maps = []
    for b in range(8):
        m = dict(common)
        m["x"] = np.ascontiguousarray(inp["x"][b])
        p = inp["p"][:, b]
        m["pT"] = np.ascontiguousarray(p.reshape(2, S, 2, 128).transpose(0, 3, 2, 1))
        in_maps.append(m)
    return in_maps


def kernel(**inputs):
    in_maps = make_in_maps(inputs)
    prog = build_program()
    res = run_bass_kernel_spmd(prog.nc, in_maps, core_ids=list(range(8)))
    out = np.stack([np.asarray(r["y"], dtype=np.float32).reshape(S, D) for r in res.results], axis=0)
    return out
```

```python
import math
import numpy as np
import concourse.bass as bass
import concourse.mybir as mybir
from concourse.bass_utils import run_bass_kernel_spmd

F32 = mybir.dt.float32
BF16 = mybir.dt.bfloat16
AF = mybir.ActivationFunctionType
ALU = mybir.AluOpType

S = 4096
D = 1024
NT = 32
NB = 8
KC = 8
FF = 2816
FC = 22
EPS = 1e-6
NEG = -30000.0
LAMBDA_INIT1 = 0.8 - 0.6 * math.exp(-0.3 * 1)
FGROUPS = [(0, 6), (6, 6), (12, 5), (17, 5)]
DA_DEPTH = 3
DA_SUM_EVERY = 2
DA_PT = 8
DA_SKIP_DVE = False


class DSem:
    def __init__(self, h):
        self.h = h
        self.count = 0


class Buf:
    def __init__(self, name, t=None):
        self.name = name
        self.t = t
        self.last_w = None
        self.readers = []
        self.dsem = None


class Ctx:
    def __init__(self, nc, n_dsems=40):
        self.nc = nc
        self.engs = {"pe": nc.tensor, "act": nc.scalar, "dve": nc.vector, "pool": nc.gpsimd, "sp": nc.sync}
        self.tl = {k: DSem(nc.alloc_semaphore(name="tl_" + k)) for k in self.engs}
        self.seen = {k: {} for k in self.engs}
        self.free_dsems = [DSem(nc.alloc_semaphore(name="d%d" % i)) for i in range(n_dsems)]
        self.all_dsems = list(self.free_dsems)
        self.live = []
        self.ninst = {k: 0 for k in self.engs}

    def push(self):
        self.live.append([])

    def pop(self):
        self.barrier()
        for cm, b in reversed(self.live.pop()):
            if b.dsem is not None:
                self.free_dsems.append(b.dsem)
            if cm is not None:
                cm.__exit__(None, None, None)

    def sb(self, name, shape, dt, dma=False):
        self.uid = getattr(self, "uid", 0) + 1
        name = "s%d_%s" % (self.uid, name)
        cm = self.nc.sbuf_tensor(name, list(shape), dt)
        b = Buf(name, cm.__enter__())
        b.shape = list(shape)
        if dma:
            b.dsem = self.free_dsems.pop()
        self.live[-1].append((cm, b))
        return b

    def ps(self, name, shape, dt=F32):
        self.uid = getattr(self, "uid", 0) + 1
        name = "p%d_%s" % (self.uid, name)
        cm = self.nc.psum_tensor(name, list(shape), dt)
        b = Buf(name, cm.__enter__())
        self.live[-1].append((cm, b))
        return b

    def dram(self, name):
        b = Buf(name)
        return b

    def ring(self, name, n, shape, dt, dma=False, psum=False):
        if psum:
            return Ring([self.ps("%s%d" % (name, i), shape, dt) for i in range(n)])
        return Ring([self.sb("%s%d" % (name, i), shape, dt, dma=dma) for i in range(n)])

    def _waits(self, eng, reads, writes):
        need = {}

        def add(tok):
            if tok is None:
                return
            s, v = tok
            if eng == "pe" and s is self.tl["pe"]:
                return
            if need.get(s, 0) < v:
                need[s] = v

        for b in reads:
            add(b.last_w)
        for b in writes:
            add(b.last_w)
            for r in b.readers:
                add(r)
        e = self.engs[eng]
        seen = self.seen[eng]
        for s, v in need.items():
            if seen.get(s, 0) < v:
                e.wait_ge(s.h, v)
                seen[s] = v

    def _record(self, tok, reads, writes):
        for b in reads:
            b.readers.append(tok)
            if len(b.readers) > 16:
                m = {}
                for s, v in b.readers:
                    if m.get(s, 0) < v:
                        m[s] = v
                b.readers = list(m.items())
        for b in writes:
            b.last_w = tok
            b.readers = []

    def op(self, eng, fn, reads=(), writes=(), inc=True):
        self._waits(eng, reads, writes)
        ins = fn(self.engs[eng])
        tl = self.tl[eng]
        if inc:
            tl.count += 1
            ins.then_inc(tl.h, 1)
            tok = (tl, tl.count)
        else:
            tok = (tl, tl.count + 1)
        self.ninst[eng] += 1
        self._record(tok, reads, writes)
        return ins

    def dma(self, q, out_ap, in_ap, owner, reads=(), writes=(), **kw):
        self._waits(q, reads, writes)
        ins = self.engs[q].dma_start(out=out_ap, in_=in_ap, **kw)
        ds = owner.dsem
        ds.count += 16
        ins.then_inc(ds.h, 16)
        tok = (ds, ds.count)
        self.ninst[q] += 1
        self._record(tok, reads, writes)
        return ins

    def barrier(self):
        sems = [self.tl[k] for k in ("pe", "act", "dve", "pool")] + self.all_dsems
        for k in self.engs:
            e = self.engs[k]
            seen = self.seen[k]
            for s in sems:
                if s.count > 0 and seen.get(s, 0) < s.count:
                    e.wait_ge(s.h, s.count)
                    seen[s] = s.count


class Ring:
    def __init__(self, bufs):
        self.bufs = bufs
        self.i = 0

    def next(self):
        b = self.bufs[self.i % len(self.bufs)]
        self.i += 1
        return b


def na_tile_list():
    tiles = [(7, c) for c in range(12, 18)]
    tiles += [(0, c) for c in range(4)] + [(15, c) for c in range(28, 32)]
    return tiles


def na_chunks(P):
    lo_row = min(max(4 * P - 4, 0), 56)
    hi_row = min(max(4 * P - 1, 0), 56) + 7
    return list(range(lo_row // 2, hi_row // 2 + 1))


def na_tile_id(P, c):
    if 1 <= P <= 14:
        return c - 2 * P + 2
    return 6 + c if P == 0 else 10 + (c - 28)


NA_NT = 14


def na_index_table():
    tiles = na_tile_list()
    idx = np.zeros((len(tiles), 128, 256), np.int64)
    kk = np.arange(128)[:, None]
    qq = np.arange(256)[None, :]
    for t, (P, c) in enumerate(tiles):
        krow = 2 * c + kk // 64
        kcol = kk % 64
        qrow = 4 * P + qq // 64
        qcol = qq % 64
        r0 = np.clip(qrow - 4, 0, 56)
        c0 = np.clip(qcol - 8, 0, 48)
        valid = (krow >= r0) & (krow < r0 + 8) & (kcol >= c0) & (kcol < c0 + 16)
        ro = krow - qrow + 7
        co = kcol - qcol + 15
        idx[t] = np.where(valid, ro * 31 + co, 15 * 31)
    return idx


def t5_bucket_np(rel):
    nn = np.abs(rel)
    nf = np.maximum(nn, 1).astype(np.float32)
    lg = (np.log(nf / np.float32(8)) / np.float32(math.log(16)) * np.float32(8)).astype(np.int32) + 8
    lg = np.minimum(lg, 15)
    return np.where(rel > 0, 16, 0) + np.where(nn < 8, nn, lg)


def ktile(w):
    K, N = w.shape
    return np.ascontiguousarray(w.reshape(K // 128, 128, N).transpose(1, 0, 2))


def prep_common(inp):
    f = np.float32
    c = {}
    g = inp["norm_g"]
    c["gT"] = np.ascontiguousarray(g.reshape(2, 5, 8, 128).transpose(3, 0, 1, 2)).astype(f)
    c["gB"] = np.ascontiguousarray(g[:, [1, 3], :]).astype(f)
    c["wqkv0"] = ktile(inp["na_w_qkv"][0])
    c["wqkv1"] = ktile(inp["da_w_qkv"][0])
    c["wo0"] = ktile(inp["na_w_o"][0])
    c["wo1"] = ktile(inp["da_w_o"][0])
    c["win"] = np.stack([ktile(inp["ffn_w_in"][l]) for l in range(2)])
    c["wout"] = np.stack([ktile(inp["ffn_w_out"][l]) for l in range(2)])
    c["wg"] = np.stack([ktile(inp["ple_w_gate"][l]) for l in range(2)])
    c["wp"] = np.stack([ktile(inp["ple_w_proj"][l]) for l in range(2)])
    cw = inp["ffn_conv_w"]
    c["cw"] = np.ascontiguousarray(cw.reshape(2, 3, FC, 128).transpose(3, 0, 1, 2)).astype(f)
    c["cb"] = np.ascontiguousarray(inp["ffn_conv_b"].reshape(2, FC, 128).transpose(2, 0, 1)).astype(f)
    rpb = inp["na_rpb"][0].reshape(16, 15 * 31)
    rpb_ext = np.concatenate([rpb, np.full((16, 1), NEG, f)], axis=1)
    idx = na_index_table()
    nab = rpb_ext[:, idx]
    c["nab"] = np.ascontiguousarray(nab.transpose(0, 2, 1, 3)).astype(f)
    kk = np.arange(128)[:, None]
    u = np.arange(1152)[None, :]
    bk = t5_bucket_np(kk - u + 512)
    tab = inp["t5_table"]
    c["t5s"] = np.ascontiguousarray(tab[bk].transpose(2, 0, 1)).astype(f)
    c["t5c"] = np.ascontiguousarray(np.broadcast_to(tab[[15, 31], :].T[None], (128, 8, 2))).astype(f)
    c["lam"] = np.ascontiguousarray(np.broadcast_to(inp["da_lambda"][0][None], (128, 4, 64))).astype(f)
    c["sg"] = np.ascontiguousarray(inp["da_subln_g"][0].reshape(128, 1)).astype(f)
    return c


COMMON_SHAPES = {
    "gT": [128, 2, 5, 8], "gB": [2, 2, D], "wqkv0": [128, KC, 3 * D], "wqkv1": [128, KC, 3 * D],
    "wo0": [128, KC, D], "wo1": [128, KC, D], "win": [2, 128, KC, 2 * FF], "wout": [2, 128, FC, D],
    "wg": [2, 128, KC, D], "wp": [2, 128, 2, D], "cw": [128, 2, 3, FC], "cb": [128, 2, FC],
    "nab": [16, 128, NA_NT, 256], "t5s": [8, 128, 1152], "t5c": [128, 8, 2], "lam": [128, 4, 64], "sg": [128, 1],
}


class Prog:
    def __init__(self, debug=False, stop_after=None, layers=(0, 1), da_heads=8, na_pairs=8):
        self.debug = debug
        self.stop_after = stop_after
        self.layers = layers
        self.da_heads = da_heads
        self.na_pairs = na_pairs
        nc = self.nc = bass.Bass("TRN2", target_bir_lowering=False)
        self.c = Ctx(nc)
        di = lambda n, s, dt=F32: nc.dram_tensor(n, list(s), dt, kind="ExternalInput").ap()
        self.x = di("x", [S, D])
        self.pT = di("pT", [2, 128, 2, S])
        self.w = {k: di(k, s) for k, s in COMMON_SHAPES.items()}
        self.y = nc.dram_tensor("y", [S, D], F32, kind="ExternalOutput").ap()
        sk = "ExternalOutput" if debug else "Internal"
        self.xs1 = nc.dram_tensor("xs1", [S, D], F32, kind=sk).ap()
        self.xs2 = nc.dram_tensor("xs2", [S, D], F32, kind=sk).ap()
        self.oT = nc.dram_tensor("oT", [8, 128, S], BF16, kind=sk).ap()
        self.aT = nc.dram_tensor("aT", [FC, 128, S], BF16, kind=sk).ap()
        c = self.c
        self.d_xs1 = c.dram("xs1")
        self.d_xs2 = c.dram("xs2")
        self.d_oT = c.dram("oT")
        self.d_aT = c.dram("aT")
        self.d_y = c.dram("y")
        self.build()

    def consts(self):
        c = self.c
        self.identf = c.sb("identf", [128, 128], F32)
        self.ident = c.sb("ident", [128, 128], BF16)
        self.ones_b = c.sb("ones_b", [128, 128], BF16)
        self.ones_f = c.sb("ones_f", [128, 128], F32)
        self.nhalf = c.sb("nhalf", [128, 512], F32)
        self.gT = c.sb("gT", [128, 2, 5, 8], F32, dma=True)
        self.cw = c.sb("cw", [128, 2, 3, FC], F32, dma=True)
        self.cb = c.sb("cb", [128, 2, FC], F32, dma=True)
        idf, idb = self.identf, self.ident
        c.op("pool", lambda e: e.memset(idf.t[:], 0.0), writes=[idf])
        c.op("pool", lambda e: e.affine_select(out=idf.t[:], in_=idf.t[:], pattern=[[-1, 128]],
                                               compare_op=ALU.not_equal, fill=1.0, base=0, channel_multiplier=1),
             reads=[idf], writes=[idf])
        c.op("dve", lambda e: e.tensor_copy(out=idb.t[:], in_=idf.t[:]), reads=[idf], writes=[idb])
        c.op("dve", lambda e: e.memset(self.ones_b.t[:], 1.0), writes=[self.ones_b])
        c.op("dve", lambda e: e.memset(self.ones_f.t[:], 1.0), writes=[self.ones_f])
        c.op("dve", lambda e: e.memset(self.nhalf.t[:], -0.5), writes=[self.nhalf])
        c.dma("sp", self.gT.t[:], self.w["gT"], self.gT, writes=[self.gT])
        c.dma("sp", self.cw.t[:], self.w["cw"], self.cw, writes=[self.cw])
        c.dma("sp", self.cb.t[:], self.w["cb"], self.cb, writes=[self.cb])

    def rstd_from_ss(self, ss_ap, out_ap, bufs, n):
        c = self.c
        c.op("dve", lambda e: e.tensor_scalar(out=out_ap, in0=ss_ap, scalar1=1.0 / n, scalar2=EPS,
                                              op0=ALU.mult, op1=ALU.add), reads=bufs, writes=bufs)
        c.op("pool", lambda e: e.tensor_tensor(out=out_ap, in0=out_ap, in1=self.nhalf.t[:, 0:1], op=ALU.pow),
             reads=list(bufs) + [self.nhalf], writes=bufs)

    def load_weight(self, dst, dst_ap_fn, src_ap_fn, nk, ncols, stage_ring, gcol_fn=None, engs=("dve",)):
        c = self.c
        scol = stage_ring.bufs[0].shape[-1]
        skc = stage_ring.bufs[0].shape[1]
        n = 0
        for k0 in range(0, nk, skc):
            k1 = min(nk, k0 + skc)
            for c0 in range(0, ncols, scol):
                c1 = min(ncols, c0 + scol)
                st = stage_ring.next()
                c.dma("sp", st.t[:, 0:k1 - k0, 0:c1 - c0], src_ap_fn(k0, k1, c0, c1), st, writes=[st])
                for k in range(k0, k1):
                    eng = engs[n % len(engs)]
                    n += 1
                    src_ap = st.t[:, k - k0, 0:c1 - c0]
                    out_ap = dst_ap_fn(k, c0, c1)
                    if gcol_fn is not None:
                        g_ap, g_buf = gcol_fn(k)
                        if eng == "act":
                            c.op("act", lambda e, o=out_ap, i=src_ap, g=g_ap: e.activation(out=o, in_=i, func=AF.Copy, scale=g),
                                 reads=[st, g_buf], writes=[dst])
                        else:
                            c.op(eng, lambda e, o=out_ap, i=src_ap, g=g_ap: e.tensor_scalar(
                                out=o, in0=i, scalar1=g, scalar2=None, op0=ALU.mult), reads=[st, g_buf], writes=[dst])
                    else:
                        if eng == "act":
                            c.op("act", lambda e, o=out_ap, i=src_ap: e.activation(out=o, in_=i, func=AF.Copy),
                                 reads=[st], writes=[dst])
                        else:
                            c.op(eng, lambda e, o=out_ap, i=src_ap: e.tensor_copy(out=o, in_=i), reads=[st], writes=[dst])

    def pipeline(self, n, stages, skews):
        order = sorted(zip(stages, skews), key=lambda fs: -fs[1])
        for i in range(n + max(skews)):
            for fn, sk in order:
                t = i - sk
                if 0 <= t < n:
                    fn(t)

    def stats(self, src_ap, src_bufs, st, col, junk):
        c = self.c
        c.op("act", lambda e: e.activation(out=junk.t[:], in_=src_ap, func=AF.Square, accum_out=st.t[:, col:col + 1]),
             reads=list(src_bufs) + [st], writes=[st])
        self.rstd_from_ss(st.t[:, col:col + 1], st.t[:, col + 1:col + 2], [st], D)

    def norm_part(self, xt, ring_junk, ring_hb, stat, si):
        c = self.c
        junk = ring_junk.bufs[0]
        c.op("act", lambda e: e.activation(out=junk.t[:], in_=xt.t[:], func=AF.Square, accum_out=stat.t[:, si:si + 1]),
             reads=[xt], writes=[stat])
        self.rstd_from_ss(stat.t[:, si:si + 1], stat.t[:, si + 1:si + 2], [stat], D)
        hb = ring_hb.next()
        c.op("act", lambda e: e.activation(out=hb.t[:], in_=xt.t[:], func=AF.Copy, scale=stat.t[:, si + 1:si + 2]),
             reads=[xt, stat], writes=[hb])
        return hb

    def transpose_part(self, hb, hT, tcol, ring_pT, eng="dve"):
        c = self.c
        pT = ring_pT.next()
        for k in range(KC):
            c.op("pe", lambda e, k=k: e.transpose(out=pT.t[:, k, :], in_=hb.t[:, k * 128:(k + 1) * 128],
                                                  identity=self.ident.t[:]),
                 reads=[hb, self.ident], writes=[pT], inc=(k == KC - 1))
        if eng == "act":
            c.op("act", lambda e: e.activation(out=hT.t[:, :, tcol:tcol + 128], in_=pT.t[:], func=AF.Copy), reads=[pT], writes=[hT])
        else:
            c.op("dve", lambda e: e.tensor_copy(out=hT.t[:, :, tcol:tcol + 128], in_=pT.t[:]), reads=[pT], writes=[hT])

    def norm_transpose(self, xt, hT, tcol, ring_junk, ring_hb, ring_pT, stat, si):
        hb = self.norm_part(xt, ring_junk, ring_hb, stat, si)
        self.transpose_part(hb, hT, tcol, ring_pT)

    def phase1(self, L, x_src, d_src, hT):
        c = self.c
        c.push()
        xr = c.ring("p1x", 6, [128, D], F32, dma=True)
        junk = c.sb("p1j", [128, D], BF16)
        hbr = c.ring("p1h", 3, [128, D], BF16)
        pT = c.ring("p1p", 2, [128, KC, 128], BF16, psum=True)
        statr = c.ring("p1s", 6, [128, 4], F32)
        xl, sts, hbs = {}, {}, {}

        def issue(t):
            xt = xr.next()
            c.dma("sp", xt.t[:], x_src[t * 128:(t + 1) * 128, :], xt, writes=[xt])
            xl[t] = xt

        def s1(t):
            if t + 4 < NT:
                issue(t + 4)
            st = statr.next()
            c.op("dve", lambda e: e.memset(st.t[:], 0.0), writes=[st])
            self.stats(xl[t].t[:], [xl[t]], st, 0, junk)
            sts[t] = st

        def s2(t):
            xt, st = xl.pop(t), sts.pop(t)
            hb = hbr.next()
            c.op("act", lambda e: e.activation(out=hb.t[:], in_=xt.t[:], func=AF.Copy, scale=st.t[:, 1:2]),
                 reads=[xt, st], writes=[hb])
            hbs[t] = hb

        def s3(t):
            self.transpose_part(hbs.pop(t), hT, t * 128, pT)

        for t in range(4):
            issue(t)
        self.pipeline(NT, [s1, s2, s3], [0, 1, 2])
        c.pop()

    def phase2_na(self, L, hT):
        c = self.c
        c.push()
        wst = c.ring("wst", 2, [128, KC, 128], F32, dma=True)
        wbr = c.ring("wb", 2, [128, KC, 384], BF16)
        QT = c.sb("QT", [128, S], BF16)
        KT = c.sb("KT", [128, 2, S], BF16)
        c.op("pool", lambda e: e.memset(KT.t[:], 0.0), writes=[KT])
        Vaug = c.sb("Vaug", [128, 2, NT, 128], BF16)
        c.op("pool", lambda e: e.memset(Vaug.t[:], 1.0), writes=[Vaug])
        nst = c.sb("nst", [128, NA_NT, 256], F32, dma=True)
        nb = [c.sb("nb%d" % a, [128, NA_NT, 256], BF16) for a in range(2)]
        PT = c.ring("PT", 3, [128, 1536], BF16)
        OTp = c.ring("OTp", 2, [128, S], BF16, dma=True)
        rr = c.ring("rr", 2, [128, 512], F32)
        pS = c.ring("pS", 2, [128, 1536], F32, psum=True)
        pO = [c.ps("pO%d" % a, [128, 512], F32) for a in range(2)]

        def v_evac(p, tb):
            pv = p.t[:, 0:512].rearrange("p (a b) -> p a b", a=4)
            c.op("dve", lambda e: e.tensor_copy(out=Vaug.t[:, 0, tb * 4:(tb + 1) * 4, 0:64], in_=pv[:, :, 0:64]),
                 reads=[p], writes=[Vaug])
            c.op("dve", lambda e: e.tensor_copy(out=Vaug.t[:, 1, tb * 4:(tb + 1) * 4, 64:128], in_=pv[:, :, 64:128]),
                 reads=[p], writes=[Vaug])

        wb_next = self.load_qkv_w(L, 0, self.w["wqkv0"], wst, wbr)
        for hp in range(self.na_pairs):
            wb = wb_next
            self.qkv_compute(wb, hT, QT, KT, None, pS, v_evac=v_evac)
            pending = []
            if hp + 1 < self.na_pairs:
                wb_next, pending = self.load_qkv_w_thunks(L, hp + 1, self.w["wqkv0"], wst, wbr)
            for a in range(2):
                h = hp * 2 + a
                c.dma("sp", nst.t[:], self.w["nab"][h], nst, writes=[nst])
                c.op("pool", lambda e, a=a: e.tensor_copy(out=nb[a].t[:], in_=nst.t[:]), reads=[nst], writes=[nb[a]])
            ot = OTp.next()
            units = [(J, a, pp) for J in range(NB) for a in range(2) for pp in range(2)]
            state = {}

            def stage_a(u):
                J, a, pp = units[u]
                P = 2 * J + pp
                chunks = na_chunks(P)
                ps = pS.next()
                n = len(chunks)
                for i, ch in enumerate(chunks):
                    tid = na_tile_id(P, ch)
                    c.op("pe", lambda e, i=i, ch=ch: e.matmul(
                        ps.t[:, i * 256:(i + 1) * 256], lhsT=KT.t[:, a, ch * 128:(ch + 1) * 128],
                        rhs=QT.t[:, P * 256:(P + 1) * 256], start=True, stop=False),
                        reads=[KT, QT], writes=[ps], inc=False)
                    c.op("pe", lambda e, i=i, tid=tid: e.matmul(
                        ps.t[:, i * 256:(i + 1) * 256], lhsT=self.ident.t[:], rhs=nb[a].t[:, tid, :],
                        start=False, stop=True), reads=[self.ident, nb[a]], writes=[ps], inc=(i == n - 1))
                pt = PT.next()
                c.op("act", lambda e: e.activation(out=pt.t[:, 0:n * 256], in_=ps.t[:, 0:n * 256], func=AF.Exp),
                     reads=[ps], writes=[pt])
                state[u] = (pt, chunks)

            def stage_b(u):
                J, a, pp = units[u]
                pt, chunks = state.pop(u)
                n = len(chunks)
                for i, ch in enumerate(chunks):
                    c.op("pe", lambda e, i=i, ch=ch: e.matmul(
                        pO[a].t[:, pp * 256:(pp + 1) * 256], lhsT=Vaug.t[:, a, ch, :], rhs=pt.t[:, i * 256:(i + 1) * 256],
                        start=(i == 0), stop=(i == n - 1)), reads=[Vaug, pt], writes=[pO[a]], inc=(i == n - 1))
                if pp == 1:
                    ol, oh = 64 * a, 64 * a + 64
                    sl, sh = 64 * (1 - a), 64 * (1 - a) + 64
                    r = rr.next()
                    c.op("dve", lambda e: e.reciprocal(out=r.t[ol:oh, :], in_=pO[a].t[sl:sh, :]),
                         reads=[pO[a]], writes=[r])
                    c.op("dve", lambda e: e.tensor_tensor(out=ot.t[ol:oh, J * 512:(J + 1) * 512], in0=pO[a].t[ol:oh, :],
                                                          in1=r.t[ol:oh, :], op=ALU.mult),
                         reads=[pO[a], r], writes=[ot])

            nu = len(units)
            stage_a(0)
            for u in range(nu):
                if u + 1 < nu:
                    stage_a(u + 1)
                stage_b(u)
                if pending:
                    pending.pop(0)()
            while pending:
                pending.pop(0)()
            c.dma("sp", self.oT[hp], ot.t[:], ot, reads=[ot])
        c.pop()

    def load_qkv_w(self, L, hp, wq_dram, wst, wbr):
        wb = wbr.next()
        for part in range(3):
            col0 = part * D + hp * 128
            self.load_weight(wb, lambda k, c0, c1, part=part: wb.t[:, k, part * 128 + c0:part * 128 + c1],
                             lambda k0, k1, c0, c1, col0=col0: wq_dram[:, k0:k1, col0 + c0:col0 + c1],
                             KC, 128, wst, gcol_fn=lambda k: (self.gT.t[:, L, 0, k:k + 1], self.gT))
        return wb

    def load_qkv_w_thunks(self, L, hp, wq_dram, wst, wbr):
        c = self.c
        wb = wbr.next()
        thunks = []
        for part in range(3):
            col0 = part * D + hp * 128
            hold = {}

            def dma_t(col0=col0, hold=hold):
                st = wst.next()
                hold["st"] = st
                c.dma("sp", st.t[:, 0:KC, 0:128], wq_dram[:, 0:KC, col0:col0 + 128], st, writes=[st])
            thunks.append(dma_t)
            for k in range(KC):
                def conv_t(k=k, part=part, hold=hold):
                    st = hold["st"]
                    c.op("dve", lambda e: e.tensor_scalar(out=wb.t[:, k, part * 128:(part + 1) * 128], in0=st.t[:, k, 0:128],
                                                          scalar1=self.gT.t[:, L, 0, k:k + 1], scalar2=None, op0=ALU.mult),
                         reads=[st, self.gT], writes=[wb])
                thunks.append(conv_t)
        return wb, thunks

    def qkv_compute(self, wb, hT, QT, KT, V, pS, view=None, v_evac=None):
        c = self.c
        if view is None:
            view = lambda p, a=0, b=512: p.t[a:b, 0:512] if False else p.t[:, 0:512]
        for tb in range(NB):
            for part, dst in ((0, QT), (1, KT)):
                p = pS.next()
                for k in range(KC):
                    c.op("pe", lambda e, k=k, part=part, p=p: e.matmul(
                        view(p), lhsT=wb.t[:, k, part * 128:(part + 1) * 128], rhs=hT.t[:, k, tb * 512:(tb + 1) * 512],
                        start=(k == 0), stop=(k == KC - 1)), reads=[wb, hT], writes=[p], inc=(k == KC - 1))
                if part == 0:
                    c.op("act", lambda e, p=p: e.activation(out=QT.t[:, tb * 512:(tb + 1) * 512], in_=view(p),
                                                            func=AF.Copy, scale=0.125), reads=[p], writes=[QT])
                else:
                    c.op("dve", lambda e, p=p: e.tensor_copy(out=KT.t[0:64, 0, tb * 512:(tb + 1) * 512], in_=view(p)[0:64, :]),
                         reads=[p], writes=[KT])
                    c.op("dve", lambda e, p=p: e.tensor_copy(out=KT.t[64:128, 1, tb * 512:(tb + 1) * 512], in_=view(p)[64:128, :]),
                         reads=[p], writes=[KT])
            p = pS.next()
            for tt in range(4):
                t = tb * 4 + tt
                for k in range(KC):
                    c.op("pe", lambda e, k=k, t=t, tt=tt, p=p: e.matmul(
                        view(p)[:, tt * 128:(tt + 1) * 128], lhsT=hT.t[:, k, t * 128:(t + 1) * 128], rhs=wb.t[:, k, 256:384],
                        start=(k == 0), stop=(k == KC - 1)), reads=[wb, hT], writes=[p], inc=(k == KC - 1 and tt == 3))
            if v_evac is not None:
                v_evac(p, tb)
            else:
                c.op("dve", lambda e, p=p: e.tensor_copy(out=V.t[:, tb * 4:(tb + 1) * 4, :],
                                                         in_=view(p).rearrange("p (a b) -> p a b", a=4)),
                     reads=[p], writes=[V])

    def phase2_da(self, L, hT):
        c = self.c
        c.push()
        wst = c.ring("wst", 2, [128, KC, 128], F32, dma=True)
        wbr = c.ring("wb", 2, [128, KC, 384], BF16)
        QT = c.sb("QT", [128, S], BF16)
        KT = c.sb("KT", [128, 2, S], BF16)
        c.op("pool", lambda e: e.memset(KT.t[:], 0.0), writes=[KT])
        V3 = c.sb("V3", [128, 3, NT, 128], BF16)
        eb = c.sb("eb", [128, 8, 2], F32)
        onesx = c.sb("onesx", [128, 2, 128], BF16)
        sst = c.sb("sst", [128, 1152], F32, dma=True)
        strip = c.sb("strip", [128, 1152], BF16)
        t5c = c.sb("t5c", [128, 8, 2], F32, dma=True)
        lam = c.sb("lam", [128, 4, 64], F32, dma=True)
        sg = c.sb("sg", [128, 1], F32, dma=True)
        lj = c.sb("lj", [128, 2, 64], F32)
        lv = c.sb("lv", [128, 8], F32)
        PT = c.ring("PT", DA_PT, [128, 512], BF16)
        OTh = c.ring("OTh", 2, [128, S], BF16, dma=True)
        tmp = c.ring("tmp", 10, [128, 512], F32)
        accr = c.ring("acc", 4, [128, 512], F32)
        SUM_PE_EVERY = DA_SUM_EVERY
        pS = c.ring("pS", DA_DEPTH + 1, [128, 512], F32, psum=True)
        pO = [c.ps("pO%d" % a, [128, 512], F32) for a in range(2)]
        pSum = [c.ps("pSum%d" % a, [128, 512], F32) for a in range(2)]
        c.dma("sp", t5c.t[:], self.w["t5c"], t5c, writes=[t5c])
        c.dma("sp", lam.t[:], self.w["lam"], lam, writes=[lam])
        c.dma("sp", sg.t[:], self.w["sg"], sg, writes=[sg])
        c.op("act", lambda e: e.activation(out=eb.t[:], in_=t5c.t[:], func=AF.Exp), reads=[t5c], writes=[eb])
        c.op("dve", lambda e: e.memset(lv.t[:], 0.0), writes=[lv])
        c.op("dve", lambda e: e.tensor_tensor(out=lj.t[:, 0, :], in0=lam.t[:, 0, :], in1=lam.t[:, 1, :], op=ALU.mult),
             reads=[lam], writes=[lj])
        c.op("dve", lambda e: e.tensor_tensor(out=lj.t[:, 1, :], in0=lam.t[:, 2, :], in1=lam.t[:, 3, :], op=ALU.mult),
             reads=[lam, lj], writes=[lj])
        c.op("dve", lambda e: e.reduce_sum(out=lv.t[:, 0:2], in_=lj.t[:], axis=mybir.AxisListType.X), reads=[lj], writes=[lv])
        c.op("act", lambda e: e.activation(out=lv.t[:, 2:4], in_=lv.t[:, 0:2], func=AF.Exp), reads=[lv], writes=[lv])
        c.op("dve", lambda e: e.tensor_tensor(out=lv.t[:, 4:5], in0=lv.t[:, 3:4], in1=lv.t[:, 2:3], op=ALU.subtract),
             reads=[lv], writes=[lv])
        c.op("dve", lambda e: e.tensor_scalar(out=lv.t[:, 4:5], in0=lv.t[:, 4:5], scalar1=-LAMBDA_INIT1, scalar2=None,
                                              op0=ALU.add), reads=[lv], writes=[lv])
        c.op("dve", lambda e: e.tensor_scalar(out=lv.t[:, 5:6], in0=sg.t[:], scalar1=1.0 - LAMBDA_INIT1, scalar2=None,
                                              op0=ALU.mult), reads=[lv, sg], writes=[lv])
        wb_next = self.load_qkv_w(L, 0, self.w["wqkv1"], wst, wbr)
        deferred = []

        def run_deferred(force=False):
            items = deferred[:]
            del deferred[:]
            for item in items:
                item[0] -= 1
                if item[0] <= 0 or force:
                    item[1]()
                else:
                    deferred.append(item)

        for h in range(self.da_heads):
            wb = wb_next

            def v_evac(p, tb, h=h):
                pv = p.t[:, 0:512].rearrange("p (a b) -> p a b", a=4)
                c.op("dve", lambda e: e.tensor_copy(out=V3.t[:, 0, tb * 4:(tb + 1) * 4, :], in_=pv), reads=[p], writes=[V3])
                for side in range(2):
                    c.op("act", lambda e, side=side: e.activation(out=V3.t[:, 1 + side, tb * 4:(tb + 1) * 4, :], in_=pv,
                                                                  func=AF.Copy, scale=eb.t[:, h, side:side + 1]),
                         reads=[p, eb], writes=[V3])
            self.qkv_compute(wb, hT, QT, KT, None, pS, v_evac=v_evac)
            for side in range(2):
                c.op("dve", lambda e, side=side: e.tensor_scalar(out=onesx.t[:, side, :], in0=self.ones_b.t[:],
                                                                 scalar1=eb.t[:, h, side:side + 1], scalar2=None, op0=ALU.mult),
                     reads=[self.ones_b, eb], writes=[onesx])
            pending = []
            if h + 1 < self.da_heads:
                wb_next, pending = self.load_qkv_w_thunks(L, h + 1, self.w["wqkv1"], wst, wbr)
            c.dma("sp", sst.t[:], self.w["t5s"][h], sst, writes=[sst])
            c.op("pool", lambda e: e.tensor_copy(out=strip.t[:], in_=sst.t[:]), reads=[sst], writes=[strip])
            ot = OTh.next()
            steps = [(J, m, kc) for J in range(NB) for m in range(2) for kc in range(NT)]
            state = {}
            tts = {}
            accs = {}

            def stage_a(s):
                J, m, kc = steps[s]
                pl, ph = 64 * m, 64 * m + 64
                off = kc * 128 - J * 512
                near = -256 < off < 640
                ps = pS.next()
                c.op("pe", lambda e: e.matmul(
                    ps.t[:], lhsT=KT.t[:, m, kc * 128:(kc + 1) * 128], rhs=QT.t[:, J * 512:(J + 1) * 512],
                    start=True, stop=not near), reads=[KT, QT], writes=[ps], inc=not near)
                if near:
                    u0 = 512 - off
                    c.op("pe", lambda e: e.matmul(
                        ps.t[:], lhsT=self.ident.t[:], rhs=strip.t[:, u0:u0 + 512], start=False, stop=True),
                        reads=[self.ident, strip], writes=[ps])
                pt = PT.next()
                c.op("act", lambda e: e.activation(out=pt.t[:], in_=ps.t[:], func=AF.Exp), reads=[ps], writes=[pt])
                state[s] = (pt, 0 if near else (1 if off < 0 else 2))

            def post_m(J, m):
                r = tmp.next()
                c.op("dve", lambda e: e.reciprocal(out=r.t[:], in_=pSum[m].t[:]), reads=[pSum[m]], writes=[r])
                t_ = tmp.next()
                c.op("dve", lambda e: e.tensor_tensor(out=t_.t[:], in0=pO[m].t[:], in1=r.t[:], op=ALU.mult),
                     reads=[pO[m], r], writes=[t_])
                tts[(J, m)] = t_

            def post_j(J, ot):
                t0_, t1_ = tts.pop((J, 0)), tts.pop((J, 1))
                o = tmp.next()
                c.op("dve", lambda e: e.scalar_tensor_tensor(out=o.t[:], in0=t1_.t[:], scalar=lv.t[:, 4:5], in1=t0_.t[:],
                                                             op0=ALU.mult, op1=ALU.add), reads=[t0_, t1_, lv], writes=[o])
                sq = tmp.next()
                c.op("pool", lambda e: e.tensor_tensor(out=sq.t[:], in0=o.t[:], in1=o.t[:], op=ALU.mult),
                     reads=[o], writes=[sq])

                def fin():
                    pst = pS.next()
                    c.op("pe", lambda e: e.matmul(pst.t[:], lhsT=self.ones_f.t[:], rhs=sq.t[:], start=True, stop=True),
                         reads=[self.ones_f, sq], writes=[pst])
                    c.op("dve", lambda e: e.tensor_scalar(out=sq.t[:], in0=pst.t[:], scalar1=1.0 / 128, scalar2=EPS,
                                                          op0=ALU.mult, op1=ALU.add), reads=[pst], writes=[sq])
                    c.op("act", lambda e: e.activation(out=sq.t[:], in_=sq.t[:], func=AF.Ln), reads=[sq], writes=[sq])
                    c.op("act", lambda e: e.activation(out=sq.t[:], in_=sq.t[:], func=AF.Exp, scale=-0.5), reads=[sq], writes=[sq])
                    c.op("dve", lambda e: e.scalar_tensor_tensor(
                        out=ot.t[:, J * 512:(J + 1) * 512], in0=o.t[:], scalar=lv.t[:, 5:6], in1=sq.t[:],
                        op0=ALU.mult, op1=ALU.mult), reads=[o, sq, lv], writes=[ot])
                deferred.append([6, fin])

            def stage_b(s, ot, h=h):
                J, m, kc = steps[s]
                pt, side = state.pop(s)
                c.op("pe", lambda e: e.matmul(
                    pO[m].t[:], lhsT=V3.t[:, side, kc, :], rhs=pt.t[:], start=(kc == 0), stop=(kc == NT - 1)),
                    reads=[V3, pt], writes=[pO[m]], inc=(kc % SUM_PE_EVERY != 0))
                if kc % SUM_PE_EVERY == 0:
                    lw = self.ones_b.t[:] if side == 0 else onesx.t[:, side - 1, :]
                    c.op("pe", lambda e: e.matmul(
                        pSum[m].t[:], lhsT=lw, rhs=pt.t[:], start=(kc == 0), stop=False),
                        reads=[self.ones_b, onesx, pt], writes=[pSum[m]])
                else:
                    first = (J, m) not in accs
                    if DA_SKIP_DVE and not first:
                        return_early = True
                    else:
                        return_early = False
                    if first:
                        accs[(J, m)] = accr.next()
                    ac = accs[(J, m)]
                    if first and side == 0:
                        c.op("dve", lambda e: e.tensor_copy(out=ac.t[:], in_=pt.t[:]), reads=[pt], writes=[ac])
                    elif first:
                        c.op("dve", lambda e: e.tensor_scalar(out=ac.t[:], in0=pt.t[:], scalar1=eb.t[:, h, side - 1:side],
                                                              scalar2=None, op0=ALU.mult), reads=[pt, eb], writes=[ac])
                    elif return_early:
                        pass
                    elif side == 0:
                        c.op("dve", lambda e: e.tensor_tensor(out=ac.t[:], in0=ac.t[:], in1=pt.t[:], op=ALU.add),
                             reads=[pt, ac], writes=[ac])
                    else:
                        c.op("dve", lambda e: e.scalar_tensor_tensor(out=ac.t[:], in0=pt.t[:], scalar=eb.t[:, h, side - 1:side],
                                                                     in1=ac.t[:], op0=ALU.mult, op1=ALU.add),
                             reads=[pt, ac, eb], writes=[ac])
                if kc == NT - 1:
                    def fin_m(J=J, m=m):
                        ac = accs.pop((J, m))
                        c.op("pe", lambda e: e.matmul(pSum[m].t[:], lhsT=self.ones_f.t[:], rhs=ac.t[:], start=False, stop=True),
                             reads=[self.ones_f, ac], writes=[pSum[m]])
                        post_m(J, m)
                        if m == 1:
                            post_j(J, ot)
                    deferred.append([3, fin_m])

            ns = len(steps)
            DEPTH = DA_DEPTH
            for s in range(min(DEPTH, ns)):
                stage_a(s)
            for s in range(ns):
                if s + DEPTH < ns:
                    stage_a(s + DEPTH)
                stage_b(s, ot)
                run_deferred()
                if pending and s % 8 == 0:
                    pending.pop(0)()
            while pending:
                pending.pop(0)()
            while deferred:
                run_deferred(force=True)
            c.dma("sp", self.oT[h], ot.t[:], ot, reads=[ot])
        c.pop()

    def phase3(self, L, x_src, d_src, hT):
        c = self.c
        c.push()
        wo = c.sb("wo", [128, KC, D], BF16)
        g1 = c.sb("g1", [128, D], F32, dma=True)
        junk = c.sb("junk", [128, D], BF16)
        wsrc = self.w["wo0"] if L == 0 else self.w["wo1"]
        c.dma("sp", g1.t[:], self.w["gB"][L, 0, :].partition_broadcast(128), g1, writes=[g1])
        c.push()
        wst = c.ring("wst", 2, [128, KC, 512], F32, dma=True)
        self.load_weight(wo, lambda k, c0, c1: wo.t[:, k, c0:c1], lambda k0, k1, c0, c1: wsrc[:, k0:k1, c0:c1],
                         KC, D, wst, engs=("dve", "act"))
        c.pop()
        OTb = c.ring("OTb", 2, [128, 8, 512], BF16, dma=True)
        xr = c.ring("xr", 5, [128, D], F32, dma=True)
        tmp = c.ring("tmp", 2, [128, D], F32)
        x1r = c.ring("x1", 4, [128, D], F32, dma=True)
        hbr = c.ring("hb", 3, [128, D], BF16)
        statr = c.ring("stat", 8, [128, 4], F32)
        pm = c.ring("pm", 3, [128, 2, 512], F32, psum=True)
        pT = c.ring("pT", 2, [128, KC, 128], BF16, psum=True)
        xl, obl, ps_, sts, x1s, hbs = {}, {}, {}, {}, {}, {}

        def issue(t):
            xt = xr.next()
            c.dma("sp", xt.t[:], x_src[t * 128:(t + 1) * 128, :], xt, writes=[xt])
            xl[t] = xt

        def issue_b(tb):
            ob = OTb.next()
            c.dma("sp", ob.t[:], self.oT[:, :, tb * 512:(tb + 1) * 512].rearrange("h p t -> p h t"), ob,
                  writes=[ob])
            obl[tb] = ob

        def s1a(t):
            tb, tt = t // 4, t % 4
            if tt == 0 and tb + 1 < NB:
                issue_b(tb + 1)
            if t + 3 < NT:
                issue(t + 3)
            ob = obl[tb]
            p = pm.next()
            for half in range(2):
                for k in range(KC):
                    c.op("pe", lambda e, k=k, half=half: e.matmul(
                        p.t[:, half, :], lhsT=ob.t[:, k, tt * 128:(tt + 1) * 128], rhs=wo.t[:, k, half * 512:(half + 1) * 512],
                        start=(k == 0), stop=(k == KC - 1)), reads=[ob, wo], writes=[p], inc=(k == KC - 1 and half == 1))
            ps_[t] = p

        def s1b(t):
            p = ps_[t]
            st = statr.next()
            c.op("dve", lambda e: e.memset(st.t[:], 0.0), writes=[st])
            self.stats(p.t[:].rearrange("p a b -> p (a b)"), [p], st, 0, junk)
            sts[t] = st

        def s2a(t):
            p, st, xt = ps_.pop(t), sts[t], xl.pop(t)
            tm = tmp.next()
            c.op("dve", lambda e: e.scalar_tensor_tensor(out=tm.t[:], in0=p.t[:].rearrange("p a b -> p (a b)"),
                                                         scalar=st.t[:, 1:2], in1=g1.t[:],
                                                         op0=ALU.mult, op1=ALU.mult), reads=[p, st, g1], writes=[tm])
            x1 = x1r.next()
            c.op("pool", lambda e: e.tensor_tensor(out=x1.t[:], in0=tm.t[:], in1=xt.t[:], op=ALU.add),
                 reads=[tm, xt], writes=[x1])
            c.dma("sp", self.xs1[t * 128:(t + 1) * 128, :], x1.t[:], x1, reads=[x1])
            x1s[t] = x1

        def s2b(t):
            self.stats(x1s[t].t[:], [x1s[t]], sts[t], 2, junk)

        def s3(t):
            x1, st = x1s.pop(t), sts.pop(t)
            hb = hbr.next()
            c.op("act", lambda e: e.activation(out=hb.t[:], in_=x1.t[:], func=AF.Copy, scale=st.t[:, 3:4]),
                 reads=[x1, st], writes=[hb])
            hbs[t] = hb

        def s4(t):
            self.transpose_part(hbs.pop(t), hT, t * 128, pT)

        issue_b(0)
        for t in range(3):
            issue(t)
        self.pipeline(NT, [s1a, s1b, s2a, s2b, s3, s4], [0, 1, 2, 3, 4, 5])
        c.pop()

    def post_norm_residual(self, p, xt, gB, xo, tmp_ring, junk_ring, stat, si):
        c = self.c
        junk = junk_ring.bufs[0]
        c.op("act", lambda e: e.activation(out=junk.t[:], in_=p.t[:].rearrange("p a b -> p (a b)"), func=AF.Square,
                                           accum_out=stat.t[:, si:si + 1]), reads=[p], writes=[stat])
        self.rstd_from_ss(stat.t[:, si:si + 1], stat.t[:, si + 1:si + 2], [stat], D)
        tm = tmp_ring.next()
        c.op("dve", lambda e: e.scalar_tensor_tensor(out=tm.t[:], in0=p.t[:].rearrange("p a b -> p (a b)"),
                                                     scalar=stat.t[:, si + 1:si + 2], in1=gB.t[:],
                                                     op0=ALU.mult, op1=ALU.mult), reads=[p, stat, gB], writes=[tm])
        c.op("pool", lambda e: e.tensor_tensor(out=xo.t[:], in0=tm.t[:], in1=xt.t[:], op=ALU.add),
             reads=[tm, xt], writes=[xo])

    def phase4a(self, L, hT):
        c = self.c
        c.push()
        wst = c.ring("wst", 2, [128, KC, 384], F32, dma=True)
        wir = c.ring("wi", 2, [128, KC, 2 * 768], BF16)
        gsb = c.ring("gsb", 2, [128, 514], F32)
        t1 = c.ring("t1", 2, [128, 512], F32)
        t2 = c.ring("t2", 2, [128, 512], F32)
        ge = c.ring("ge", 2, [128, 512], F32)
        aTb = c.ring("aTb", 2, [128, 6, 512], BF16, dma=True)
        pg = c.ring("pg", 2, [128, 512], F32, psum=True)
        pv = c.ring("pv", 2, [128, 512], F32, psum=True)
        win = self.w["win"][L]
        def load_group(gi):
            fc0, nf = FGROUPS[gi]
            wi = wir.next()
            for part in range(2):
                col0 = part * FF + fc0 * 128
                self.load_weight(wi, lambda k, c0, c1, part=part: wi.t[:, k, part * 768 + c0:part * 768 + c1],
                                 lambda k0, k1, c0, c1, col0=col0: win[:, k0:k1, col0 + c0:col0 + c1],
                                 KC, nf * 128, wst, gcol_fn=lambda k: (self.gT.t[:, L, 2, k:k + 1], self.gT), engs=("act",))
            return wi
        BW = 510
        blocks = [(s0, min(BW, S - s0)) for s0 in range(0, S, BW)]
        def load_group_thunks(gi):
            fc0_, nf_ = FGROUPS[gi]
            wi_ = wir.next()
            thunks = []
            for part in range(2):
                col0 = part * FF + fc0_ * 128
                for c0 in range(0, nf_ * 128, 384):
                    c1 = min(nf_ * 128, c0 + 384)
                    hold = {}

                    def dma_t(col0=col0, c0=c0, c1=c1, hold=hold):
                        st = wst.next()
                        hold["st"] = st
                        c.dma("sp", st.t[:, 0:KC, 0:c1 - c0], win[:, 0:KC, col0 + c0:col0 + c1], st, writes=[st])
                    thunks.append(dma_t)
                    for k in range(KC):
                        def conv_t(k=k, part=part, c0=c0, c1=c1, hold=hold):
                            st = hold["st"]
                            c.op("act", lambda e: e.activation(out=wi_.t[:, k, part * 768 + c0:part * 768 + c1],
                                                               in_=st.t[:, k, 0:c1 - c0], func=AF.Copy,
                                                               scale=self.gT.t[:, L, 2, k:k + 1]),
                                 reads=[st, self.gT], writes=[wi_])
                        thunks.append(conv_t)
            return wi_, thunks

        wi_next = load_group(0)
        for gi, (fc0, nf) in enumerate(FGROUPS):
            wi = wi_next
            pending = []
            if gi + 1 < len(FGROUPS):
                wi_next, pending = load_group_thunks(gi + 1)
            for (s0, W) in blocks:
                glo, ghi = max(s0 - 1, 0), min(s0 + W + 1, S)
                gn = ghi - glo
                goff = glo - (s0 - 1)
                ab = aTb.next()
                for fi in range(nf):
                    fc = fc0 + fi
                    if pending:
                        pending.pop(0)()
                    g_ps = pg.next()
                    v_ps = pv.next()
                    for k in range(KC):
                        c.op("pe", lambda e, k=k: e.matmul(
                            g_ps.t[:, 0:gn], lhsT=wi.t[:, k, fi * 128:(fi + 1) * 128], rhs=hT.t[:, k, glo:ghi],
                            start=(k == 0), stop=(k == KC - 1)), reads=[wi, hT], writes=[g_ps], inc=(k == KC - 1))
                    for k in range(KC):
                        c.op("pe", lambda e, k=k: e.matmul(
                            v_ps.t[:, 0:W], lhsT=wi.t[:, k, 768 + fi * 128:768 + (fi + 1) * 128], rhs=hT.t[:, k, s0:s0 + W],
                            start=(k == 0), stop=(k == KC - 1)), reads=[wi, hT], writes=[v_ps], inc=(k == KC - 1))
                    gs = gsb.next()
                    c.op("act", lambda e: e.activation(out=gs.t[:, goff:goff + gn], in_=g_ps.t[:, 0:gn], func=AF.Copy),
                         reads=[g_ps], writes=[gs])
                    if goff == 1:
                        c.op("dve", lambda e: e.memset(gs.t[:, 0:1], 0.0), writes=[gs])
                    if goff + gn < W + 2:
                        c.op("dve", lambda e: e.memset(gs.t[:, W + 1:W + 2], 0.0), writes=[gs])
                    a1 = t1.next()
                    a2 = t2.next()
                    cwv = lambda j_: self.cw.t[:, L, j_, fc:fc + 1]
                    c.op("dve", lambda e: e.tensor_scalar(out=a1.t[:, 0:W], in0=gs.t[:, 0:W], scalar1=cwv(0),
                                                          scalar2=None, op0=ALU.mult),
                         reads=[gs, self.cw], writes=[a1])
                    c.op("dve", lambda e: e.scalar_tensor_tensor(
                        out=a2.t[:, 0:W], in0=gs.t[:, 1:W + 1], scalar=cwv(1), in1=a1.t[:, 0:W], op0=ALU.mult, op1=ALU.add),
                        reads=[gs, a1, self.cw], writes=[a2])
                    c.op("dve", lambda e: e.scalar_tensor_tensor(
                        out=a1.t[:, 0:W], in0=gs.t[:, 2:W + 2], scalar=cwv(2), in1=a2.t[:, 0:W], op0=ALU.mult, op1=ALU.add),
                        reads=[gs, a2, self.cw], writes=[a1])
                    gg = ge.next()
                    c.op("act", lambda e: e.activation(out=gg.t[:, 0:W], in_=a1.t[:, 0:W], func=AF.Gelu_apprx_tanh,
                                                       bias=self.cb.t[:, L, fc:fc + 1]),
                         reads=[a1, self.cb], writes=[gg])
                    c.op("dve", lambda e: e.tensor_tensor(out=ab.t[:, fi, 0:W], in0=v_ps.t[:, 0:W], in1=gg.t[:, 0:W], op=ALU.mult),
                         reads=[gg, v_ps], writes=[ab])
                c.dma("sp", self.aT[fc0:fc0 + nf, :, s0:s0 + W].rearrange("f p t -> p f t"), ab.t[:, 0:nf, 0:W], ab,
                      reads=[ab])
            while pending:
                pending.pop(0)()
        c.pop()

    def phase4b(self, L, y_dst, d_dst):
        c = self.c
        c.push()
        wout = c.sb("wout", [128, FC, D], BF16)
        wg = c.sb("wg", [128, KC, D], BF16)
        wp = c.sb("wp", [128, 2, D], BF16)
        g3 = c.sb("g3", [128, D], F32, dma=True)
        junk = c.sb("junk", [128, D], BF16)
        c.dma("sp", g3.t[:], self.w["gB"][L, 1, :].partition_broadcast(128), g3, writes=[g3])
        wo_src, wg_src, wp_src = self.w["wout"][L], self.w["wg"][L], self.w["wp"][L]
        c.push()
        wst = c.ring("wst", 2, [128, 2, 1024], F32, dma=True)
        self.load_weight(wout, lambda k, c0, c1: wout.t[:, k, c0:c1], lambda k0, k1, c0, c1: wo_src[:, k0:k1, c0:c1],
                         FC, D, wst, engs=("dve", "act"))
        self.load_weight(wg, lambda k, c0, c1: wg.t[:, k, c0:c1], lambda k0, k1, c0, c1: wg_src[:, k0:k1, c0:c1],
                         KC, D, wst, gcol_fn=lambda k: (self.gT.t[:, L, 4, k:k + 1], self.gT), engs=("dve", "act"))
        self.load_weight(wp, lambda k, c0, c1: wp.t[:, k, c0:c1], lambda k0, k1, c0, c1: wp_src[:, k0:k1, c0:c1],
                         2, D, wst, engs=("dve", "act"))
        c.pop()
        aTb = c.ring("aTb", 2, [128, FC, 512], BF16, dma=True)
        pst = c.ring("pst", 2, [128, 2, 512], F32, dma=True)
        pb = c.ring("pb", 3, [128, 2, 512], BF16)
        xr = c.ring("xr", 4, [128, D], F32, dma=True)
        tmp = c.ring("tmp", 2, [128, D], F32)
        x2r = c.ring("x2", 5, [128, D], F32)
        x3r = c.ring("x3", 2, [128, D], F32, dma=True)
        sgr = c.ring("sig", 2, [128, D], F32)
        hbr = c.ring("hb", 3, [128, D], BF16)
        h3T = c.ring("h3T", 2, [128, KC, 128], BF16)
        statr = c.ring("stat", 8, [128, 4], F32)
        pf = c.ring("pf", 2, [128, 2, 512], F32, psum=True)
        pT = c.ring("pT", 1, [128, KC, 128], BF16, psum=True)
        pgp = c.ring("pgp", 3, [128, 512], F32, psum=True)
        xl, bl, ps_, sts, x2s, hbs, h3s = {}, {}, {}, {}, {}, {}, {}

        def issue(t):
            xt = xr.next()
            c.dma("sp", xt.t[:], self.xs1[t * 128:(t + 1) * 128, :], xt, writes=[xt])
            xl[t] = xt

        def issue_b(tb):
            t0 = tb * 512
            ab = aTb.next()
            for f0 in range(0, FC, 8):
                f1 = min(FC, f0 + 8)
                c.dma("sp", ab.t[:, f0:f1, :], self.aT[f0:f1, :, t0:t0 + 512].rearrange("f p t -> p f t"), ab,
                      writes=[ab])
            pq = pst.next()
            c.dma("sp", pq.t[:], self.pT[L, :, :, t0:t0 + 512], pq, writes=[pq])
            pbb = pb.next()
            c.op("pool", lambda e: e.tensor_copy(out=pbb.t[:], in_=pq.t[:]), reads=[pq], writes=[pbb])
            bl[tb] = (ab, pbb)

        def s1(t):
            tb, tt = t // 4, t % 4
            if tt == 0 and tb + 1 < NB:
                issue_b(tb + 1)
            if t + 2 < NT:
                issue(t + 2)
            ab, pbb = bl[tb]
            p = pf.next()
            for half in range(2):
                for k in range(FC):
                    c.op("pe", lambda e, k=k, half=half: e.matmul(
                        p.t[:, half, :], lhsT=ab.t[:, k, tt * 128:(tt + 1) * 128], rhs=wout.t[:, k, half * 512:(half + 1) * 512],
                        start=(k == 0), stop=(k == FC - 1)), reads=[ab, wout], writes=[p], inc=(k == FC - 1 and half == 1))
            st = statr.next()
            c.op("dve", lambda e: e.memset(st.t[:], 0.0), writes=[st])
            self.stats(p.t[:].rearrange("p a b -> p (a b)"), [p], st, 0, junk)
            ps_[t], sts[t] = p, st

        def s2(t):
            p, st, xt = ps_.pop(t), sts[t], xl.pop(t)
            tm = tmp.next()
            c.op("dve", lambda e: e.scalar_tensor_tensor(out=tm.t[:], in0=p.t[:].rearrange("p a b -> p (a b)"),
                                                         scalar=st.t[:, 1:2], in1=g3.t[:],
                                                         op0=ALU.mult, op1=ALU.mult), reads=[p, st, g3], writes=[tm])
            x2 = x2r.next()
            c.op("pool", lambda e: e.tensor_tensor(out=x2.t[:], in0=tm.t[:], in1=xt.t[:], op=ALU.add),
                 reads=[tm, xt], writes=[x2])
            self.stats(x2.t[:], [x2], st, 2, junk)
            x2s[t] = x2

        def s3(t):
            x2, st = x2s[t], sts.pop(t)
            hb = hbr.next()
            c.op("act", lambda e: e.activation(out=hb.t[:], in_=x2.t[:], func=AF.Copy, scale=st.t[:, 3:4]),
                 reads=[x2, st], writes=[hb])
            hbs[t] = hb

        def s4(t):
            h3 = h3T.next()
            self.transpose_part(hbs.pop(t), h3, 0, pT, eng="act")
            h3s[t] = h3

        def s5(t):
            tb, tt = t // 4, t % 4
            x2, h3 = x2s.pop(t), h3s.pop(t)
            pbb = bl[tb][1]
            x3 = x3r.next()
            for half in range(2):
                pgate = pgp.next()
                for k in range(KC):
                    c.op("pe", lambda e, k=k: e.matmul(
                        pgate.t[:], lhsT=h3.t[:, k, :], rhs=wg.t[:, k, half * 512:(half + 1) * 512],
                        start=(k == 0), stop=(k == KC - 1)), reads=[h3, wg], writes=[pgate], inc=(k == KC - 1))
                sg_ = sgr.next()
                c.op("act", lambda e: e.activation(out=sg_.t[:, 0:512], in_=pgate.t[:], func=AF.Sigmoid),
                     reads=[pgate], writes=[sg_])
                pproj = pgp.next()
                for k in range(2):
                    c.op("pe", lambda e, k=k: e.matmul(
                        pproj.t[:], lhsT=pbb.t[:, k, tt * 128:(tt + 1) * 128], rhs=wp.t[:, k, half * 512:(half + 1) * 512],
                        start=(k == 0), stop=(k == 1)), reads=[pbb, wp], writes=[pproj], inc=(k == 1))
                c.op("dve", lambda e: e.tensor_tensor(out=sg_.t[:, 512:1024], in0=pproj.t[:], in1=sg_.t[:, 0:512], op=ALU.mult),
                     reads=[pproj, sg_], writes=[sg_])
                c.op("pool", lambda e: e.tensor_tensor(
                    out=x3.t[:, half * 512:(half + 1) * 512], in0=sg_.t[:, 512:1024], in1=x2.t[:, half * 512:(half + 1) * 512],
                    op=ALU.add), reads=[sg_, x2], writes=[x3])
            c.dma("sp", y_dst[t * 128:(t + 1) * 128, :], x3.t[:], x3, reads=[x3])

        issue_b(0)
        for t in range(2):
            issue(t)
        self.pipeline(NT, [s1, s2, s3, s4, s5], [0, 1, 2, 3, 4])
        c.pop()

    def build(self):
        c = self.c
        c.push()
        self.consts()
        srcs = [(self.x, None), (self.xs2, self.d_xs2)]
        dsts = [(self.xs2, self.d_xs2), (self.y, self.d_y)]
        for L in self.layers:
            x_src, d_src = srcs[L]
            if len(self.layers) == 1:
                x_src = self.x
            c.push()
            hT = c.sb("hT", [128, KC, S], BF16)
            self.phase1(L, x_src, d_src, hT)
            if self.stop_after == (L, 1):
                self.dump_hT(hT)
                c.pop(); break
            if L == 0:
                self.phase2_na(L, hT)
            else:
                self.phase2_da(L, hT)
            if self.stop_after == (L, 2):
                c.pop(); break
            self.phase3(L, x_src, d_src, hT)
            if self.stop_after == (L, 3):
                c.pop(); break
            self.phase4a(L, hT)
            c.pop()
            if self.stop_after == (L, 4):
                break
            y_dst, d_dst = dsts[L]
            self.phase4b(L, y_dst, d_dst)
            if self.stop_after == (L, 5):
                break
        c.pop()

    def dump_hT(self, hT):
        pass


def build_program(debug=False, stop_after=None, **kw):
    return Prog(debug=debug, stop_after=stop_after, **kw)


def make_in_maps(inputs):
    inp = {k: np.asarray(v, dtype=np.float32) for k, v in inputs.items()}
    common = prep_common(inp)
    in_maps = []
    for b in range(8):
        m = dict(common)
        m["x"] = np.ascontiguousarray(inp["x"][b])
        p = inp["p"][:, b]
        m["pT"] = np.ascontiguousarray(p.reshape(2, S, 2, 128).transpose(0, 3, 2, 1))
        in_maps.append(m)
    return in_maps


def kernel(**inputs):
    in_maps = make_in_maps(inputs)
    prog = build_program()
    res = run_bass_kernel_spmd(prog.nc, in_maps, core_ids=list(range(8)))
    out = np.stack([np.asarray(r["y"], dtype=np.float32).reshape(S, D) for r in res.results], axis=0)
    return out
```
